# Optimizing a Trainium2 kernel written in Bass

```python
import jax, jax.numpy as jnp
from jax import lax
import numpy as np

D_MODEL = 1024
BATCH = 8
SEQ = 4096
DEPTH = 2

HEAD_DIM = 64
ROPE_THETA = 500000.0
PARTIAL_ROT = HEAD_DIM // 4
NORM_EPS = 1e-6
Q_BLOCK = 128

NSA_HEADS = 8
NSA_GROUPS = 2
NSA_REP = NSA_HEADS // NSA_GROUPS
CMP_LEN = 32
CMP_STRIDE = 16
CMP_HID = 4 * HEAD_DIM
SLC_LEN = 64
SLC_TOPK = 16
WIN = 512
NSA_Q_BLOCK = 32

FOX_HEADS = 8

MLA_HEADS = 8
MLA_Q_LORA = 384
MLA_KV_LORA = 256
MLA_NOPE = 64
MLA_ROPE = 32
MLA_V = 64

D_FF = 2816
CONV_W = 3

NSA_W = NSA_HEADS * HEAD_DIM
NSA_KV_W = NSA_GROUPS * HEAD_DIM
FOX_W = FOX_HEADS * HEAD_DIM
MLA_W = MLA_HEADS * MLA_V

IN_SIZES = (
    NSA_W,
    6 * NSA_KV_W,
    3 * NSA_HEADS,
    3 * FOX_W,
    FOX_HEADS,
    MLA_Q_LORA,
    MLA_KV_LORA,
    MLA_ROPE,
    3 * D_MODEL,
)
N_IN = NSA_W + 6 * NSA_KV_W + 3 * NSA_HEADS + 3 * FOX_W + FOX_HEADS + MLA_Q_LORA + MLA_KV_LORA + MLA_ROPE + 3 * D_MODEL

kernel_name = "hybrid_nsa_fox_mla_gated_convffn"


def rms_norm(x, g):
    xf = x.astype(jnp.float32)
    y = xf * lax.rsqrt(jnp.mean(xf * xf, axis=-1, keepdims=True) + NORM_EPS)
    return (y * g.astype(jnp.float32)).astype(x.dtype)


def rotary(x, positions, rot_dim):
    half = rot_dim // 2
    inv_freq = ROPE_THETA ** (-jnp.arange(half, dtype=jnp.float32) / half)
    ang = positions.astype(jnp.float32)[..., None] * inv_freq
    ang = ang.reshape(ang.shape[:2] + (1,) * (x.ndim - 3) + (half,))
    cos, sin = jnp.cos(ang), jnp.sin(ang)
    xr = x[..., :rot_dim].astype(jnp.float32)
    x1, x2 = xr[..., :half], xr[..., half:]
    rot = jnp.concatenate([x1 * cos - x2 * sin, x2 * cos + x1 * sin], axis=-1).astype(x.dtype)
    return jnp.concatenate([rot, x[..., rot_dim:]], axis=-1)


def masked_softmax(s, mask):
    s = jnp.where(mask, s.astype(jnp.float32), -jnp.inf)
    m = jnp.max(s, axis=-1, keepdims=True)
    m = jnp.where(jnp.isfinite(m), m, 0.0)
    p = jnp.exp(s - m)
    return p / jnp.maximum(jnp.sum(p, axis=-1, keepdims=True), 1e-30)


def causal_attention_blocked(q, k, v, scale, log_decay=None):
    B, S, H, _ = q.shape
    dv = v.shape[-1]
    k_pos = jnp.arange(S)
    F = None if log_decay is None else jnp.transpose(log_decay, (0, 2, 1))

    def block(i):
        t0 = i * Q_BLOCK
        qb = lax.dynamic_slice_in_dim(q, t0, Q_BLOCK, axis=1)
        s = jnp.einsum('bqhd,bshd->bhqs', qb, k).astype(jnp.float32) * scale
        if F is not None:
            fq = lax.dynamic_slice_in_dim(F, t0, Q_BLOCK, axis=2)
            s = s + fq[..., None] - F[:, :, None, :]
        q_pos = t0 + jnp.arange(Q_BLOCK)
        p = masked_softmax(s, k_pos[None, :] <= q_pos[:, None])
        return jnp.einsum('bhqs,bshd->bqhd', p.astype(v.dtype), v)

    out = lax.map(block, jnp.arange(S // Q_BLOCK))
    return jnp.moveaxis(out, 0, 1).reshape(B, S, H, dv)


def nsa_mixer(q, kv, gate_logits, positions, pe_k, w1_k, w2_k, pe_v, w1_v, w2_v):
    B, S, _ = q.shape
    G, R, Dh = NSA_GROUPS, NSA_REP, HEAD_DIM
    scale = Dh ** -0.5
    q = q.reshape(B, S, G, R, Dh)
    k_cmp, v_cmp, k_slc, v_slc, k_win, v_win = [t.reshape(B, S, G, Dh) for t in jnp.split(kv, 6, axis=-1)]

    n_cmp = (S - CMP_LEN) // CMP_STRIDE + 1
    blk_idx = CMP_STRIDE * np.arange(n_cmp)[:, None] + np.arange(CMP_LEN)[None, :]

    def compress(t, pe, w1, w2):
        blocks = t[:, blk_idx] + pe[:, None, :]
        blocks = jnp.transpose(blocks, (0, 1, 3, 2, 4)).reshape(B, n_cmp, G, CMP_LEN * Dh)
        return jax.nn.gelu(blocks @ w1) @ w2

    kc = compress(k_cmp, pe_k, w1_k, w2_k)
    vc = compress(v_cmp, pe_v, w1_v, w2_v)
    t_np = np.arange(S)
    cmp_end = CMP_STRIDE * np.arange(n_cmp) + CMP_LEN - 1
    s_cmp = jnp.einsum('bsgrd,bngd->bgrsn', q, kc).astype(jnp.float32) * scale
    p_cmp = masked_softmax(s_cmp, cmp_end[None, :] <= t_np[:, None])
    o_cmp = jnp.einsum('bgrsn,bngd->bsgrd', p_cmp.astype(vc.dtype), vc)

    n_slc = S // SLC_LEN
    c0 = CMP_STRIDE * np.arange(n_cmp)[:, None]
    s0 = SLC_LEN * np.arange(n_slc)[None, :]
    overlap = np.clip(np.minimum(c0 + CMP_LEN, s0 + SLC_LEN) - np.maximum(c0, s0), 0, None) / CMP_LEN
    imp = jnp.einsum('bgrsn,nj->bgsj', p_cmp, jnp.asarray(overlap, jnp.float32))
    cur = t_np // SLC_LEN
    j = np.arange(n_slc)
    forced = (j[None, :] == 0) | (j[None, :] == cur[:, None]) | (j[None, :] == cur[:, None] - 1)
    future = j[None, :] > cur[:, None]
    imp = jnp.where(forced, 1e9, jnp.where(future, -1e9, imp))
    n_top = min(SLC_TOPK, n_slc)
    _, sel = lax.top_k(imp, n_top)

    q_r = rotary(q, positions, PARTIAL_ROT)
    k_slc = rotary(k_slc, positions, PARTIAL_ROT)
    k_win = rotary(k_win, positions, PARTIAL_ROT)
    k_slc_b = jnp.transpose(k_slc.reshape(B, n_slc, SLC_LEN, G, Dh), (0, 3, 1, 2, 4))
    v_slc_b = jnp.transpose(v_slc.reshape(B, n_slc, SLC_LEN, G, Dh), (0, 3, 1, 2, 4))
    pad = ((0, 0), (WIN, 0), (0, 0), (0, 0))
    k_win_p = jnp.pad(k_win, pad)
    v_win_p = jnp.pad(v_win, pad)
    b_ix = jnp.arange(B)[:, None, None, None]
    g_ix = jnp.arange(G)[None, :, None, None]
    QB = NSA_Q_BLOCK

    def block(i):
        t0 = i * QB
        qb = lax.dynamic_slice_in_dim(q_r, t0, QB, axis=1)
        tq = t0 + jnp.arange(QB)
        idx = lax.dynamic_slice_in_dim(sel, t0, QB, axis=2)
        kg = k_slc_b[b_ix, g_ix, idx]
        vg = v_slc_b[b_ix, g_ix, idx]
        s = jnp.einsum('bqgrd,bgqnld->bgrqnl', qb, kg).astype(jnp.float32) * scale
        s_pos = idx[..., None] * SLC_LEN + jnp.arange(SLC_LEN)
        m_sel = (s_pos <= tq[None, None, :, None, None]).reshape(B, G, 1, QB, n_top * SLC_LEN)
        p = masked_softmax(s.reshape(B, G, R, QB, n_top * SLC_LEN), m_sel)
        o_s = jnp.einsum('bgrqk,bgqkd->bqgrd', p.astype(vg.dtype), vg.reshape(B, G, QB, n_top * SLC_LEN, Dh))
        kw = lax.dynamic_slice_in_dim(k_win_p, t0, QB + WIN, axis=1)
        vw = lax.dynamic_slice_in_dim(v_win_p, t0, QB + WIN, axis=1)
        s_w = jnp.einsum('bqgrd,bkgd->bgrqk', qb, kw).astype(jnp.float32) * scale
        kpos = t0 - WIN + jnp.arange(QB + WIN)
        diff = tq[:, None] - kpos[None, :]
        m_win = (kpos[None, :] >= 0) & (diff >= 0) & (diff < WIN)
        p_w = masked_softmax(s_w, m_win)
        o_w = jnp.einsum('bgrqk,bkgd->bqgrd', p_w.astype(vw.dtype), vw)
        return o_s, o_w

    o_slc, o_win = lax.map(block, jnp.arange(S // QB))
    o_slc = jnp.moveaxis(o_slc, 0, 1).reshape(B, S, G, R, Dh)
    o_win = jnp.moveaxis(o_win, 0, 1).reshape(B, S, G, R, Dh)
    g = jax.nn.sigmoid(gate_logits.astype(jnp.float32)).astype(q.dtype).reshape(B, S, G, R, 3)
    o = g[..., 0:1] * o_cmp + g[..., 1:2] * o_slc + g[..., 2:3] * o_win
    return o.reshape(B, S, NSA_W)


def fox_mixer(qkv, f_logits, b_forget):
    B, S, _ = qkv.shape
    q, k, v = [t.reshape(B, S, FOX_HEADS, HEAD_DIM) for t in jnp.split(qkv, 3, axis=-1)]
    log_f = jax.nn.log_sigmoid(f_logits.astype(jnp.float32) + b_forget.astype(jnp.float32))
    F = jnp.cumsum(log_f, axis=1)
    o = causal_attention_blocked(q, k, v, HEAD_DIM ** -0.5, F)
    return o.reshape(B, S, FOX_W)


def mla_mixer(c_q, c_kv, k_rope, positions, q_norm, w_uq, kv_norm, w_ukv):
    B, S, _ = c_q.shape
    H = MLA_HEADS
    q = (rms_norm(c_q, q_norm) @ w_uq).reshape(B, S, H, MLA_NOPE + MLA_ROPE)
    kv = (rms_norm(c_kv, kv_norm) @ w_ukv).reshape(B, S, H, MLA_NOPE + MLA_V)
    q_nope, q_pe = q[..., :MLA_NOPE], rotary(q[..., MLA_NOPE:], positions, MLA_ROPE)
    k_nope, v = kv[..., :MLA_NOPE], kv[..., MLA_NOPE:]
    k_pe = rotary(k_rope, positions, MLA_ROPE)
    qf = jnp.concatenate([q_nope, q_pe], axis=-1)
    kf = jnp.concatenate([k_nope, jnp.broadcast_to(k_pe[:, :, None, :], (B, S, H, MLA_ROPE))], axis=-1)
    o = causal_attention_blocked(qf, kf, v, (MLA_NOPE + MLA_ROPE) ** -0.5)
    return o.reshape(B, S, MLA_W)


def hybrid_mixer(x, positions, norm, w_in, b_forget, pe_k, w1_k, w2_k, pe_v, w1_v, w2_v,
                 q_norm, w_uq, kv_norm, w_ukv, w_br_nsa, w_br_fox, w_br_mla, w_out):
    B, S, D = x.shape
    h = rms_norm(x, norm)
    proj = h @ w_in
    (nsa_q, nsa_kv, nsa_g, fox_qkv, fox_f, mla_cq, mla_ckv, mla_kr, merge_g) = jnp.split(
        proj, np.cumsum(IN_SIZES)[:-1].tolist(), axis=-1)
    o_nsa = nsa_mixer(nsa_q, nsa_kv, nsa_g, positions, pe_k, w1_k, w2_k, pe_v, w1_v, w2_v)
    o_fox = fox_mixer(fox_qkv, fox_f, b_forget)
    o_mla = mla_mixer(mla_cq, mla_ckv, mla_kr, positions, q_norm, w_uq, kv_norm, w_ukv)
    gates = jax.nn.sigmoid(merge_g.astype(jnp.float32)).astype(x.dtype).reshape(B, S, 3, D)
    merged = (gates[:, :, 0] * (o_nsa @ w_br_nsa)
              + gates[:, :, 1] * (o_fox @ w_br_fox)
              + gates[:, :, 2] * (o_mla @ w_br_mla))
    return merged @ w_out


def conv_ffn(x, norm, w_up, conv_w, conv_b, w_down):
    S = x.shape[1]
    h = rms_norm(x, norm)
    u, v = jnp.split(h @ w_up, 2, axis=-1)
    up = jnp.pad(u, ((0, 0), (CONV_W - 1, 0), (0, 0)))
    uc = sum(conv_w[k] * up[:, k:k + S] for k in range(CONV_W)) + conv_b
    return (jax.nn.silu(uc) * v) @ w_down


def setup_inputs(seed: int = 0) -> dict:
    key = jax.random.key(seed)
    ks = jax.random.split(key, 32)
    f32 = jnp.float32
    L = DEPTH

    def nrm(i, shape, scale):
        return jax.random.normal(ks[i], shape, f32) * scale

    x = nrm(0, (BATCH, SEQ, D_MODEL), 1.0)
    positions = (jax.random.randint(ks[1], (BATCH, 1), 0, 1024, dtype=jnp.int32)
                 + jnp.arange(SEQ, dtype=jnp.int32)[None, :])
    return {
        "x": x,
        "positions": positions,
        "mix_norm": 1.0 + nrm(2, (L, D_MODEL), 0.02),
        "w_in": nrm(3, (L, D_MODEL, N_IN), D_MODEL ** -0.5),
        "b_forget": jax.random.uniform(ks[4], (L, FOX_HEADS), f32, 2.0, 6.0),
        "cmp_pe_k": nrm(5, (L, CMP_LEN, HEAD_DIM), 0.1),
        "cmp_w1_k": nrm(6, (L, CMP_LEN * HEAD_DIM, CMP_HID), (CMP_LEN * HEAD_DIM) ** -0.5),
        "cmp_w2_k": nrm(7, (L, CMP_HID, HEAD_DIM), CMP_HID ** -0.5),
        "cmp_pe_v": nrm(8, (L, CMP_LEN, HEAD_DIM), 0.1),
        "cmp_w1_v": nrm(9, (L, CMP_LEN * HEAD_DIM, CMP_HID), (CMP_LEN * HEAD_DIM) ** -0.5),
        "cmp_w2_v": nrm(10, (L, CMP_HID, HEAD_DIM), CMP_HID ** -0.5),
        "mla_q_norm": 1.0 + nrm(11, (L, MLA_Q_LORA), 0.02),
        "mla_w_uq": nrm(12, (L, MLA_Q_LORA, MLA_HEADS * (MLA_NOPE + MLA_ROPE)), MLA_Q_LORA ** -0.5),
        "mla_kv_norm": 1.0 + nrm(13, (L, MLA_KV_LORA), 0.02),
        "mla_w_ukv": nrm(14, (L, MLA_KV_LORA, MLA_HEADS * (MLA_NOPE + MLA_V)), MLA_KV_LORA ** -0.5),
        "w_br_nsa": nrm(15, (L, NSA_W, D_MODEL), NSA_W ** -0.5),
        "w_br_fox": nrm(16, (L, FOX_W, D_MODEL), FOX_W ** -0.5),
        "w_br_mla": nrm(17, (L, MLA_W, D_MODEL), MLA_W ** -0.5),
        "w_out": nrm(18, (L, D_MODEL, D_MODEL), D_MODEL ** -0.5),
        "ffn_norm": 1.0 + nrm(19, (L, D_MODEL), 0.02),
        "w_up": nrm(20, (L, D_MODEL, 2 * D_FF), D_MODEL ** -0.5),
        "conv_w": nrm(21, (L, CONV_W, D_FF), CONV_W ** -0.5),
        "conv_b": nrm(22, (L, D_FF), 0.02),
        "w_down": nrm(23, (L, D_FF, D_MODEL), D_FF ** -0.5),
        "final_norm": 1.0 + nrm(24, (D_MODEL,), 0.02),
    }


def reference(x, positions, mix_norm, w_in, b_forget, cmp_pe_k, cmp_w1_k, cmp_w2_k,
              cmp_pe_v, cmp_w1_v, cmp_w2_v, mla_q_norm, mla_w_uq, mla_kv_norm, mla_w_ukv,
              w_br_nsa, w_br_fox, w_br_mla, w_out, ffn_norm, w_up, conv_w, conv_b, w_down,
              final_norm):
    h = x
    for l in range(DEPTH):
        h = h + hybrid_mixer(h, positions, mix_norm[l], w_in[l], b_forget[l],
                             cmp_pe_k[l], cmp_w1_k[l], cmp_w2_k[l],
                             cmp_pe_v[l], cmp_w1_v[l], cmp_w2_v[l],
                             mla_q_norm[l], mla_w_uq[l], mla_kv_norm[l], mla_w_ukv[l],
                             w_br_nsa[l], w_br_fox[l], w_br_mla[l], w_out[l])
        h = h + conv_ffn(h, ffn_norm[l], w_up[l], conv_w[l], conv_b[l], w_down[l])
    return rms_norm(h, final_norm)
```

```python
import contextlib
import math
import numpy as np
import ml_dtypes
import concourse.bass as bass
import concourse.mybir as mybir
from concourse.bass_utils import run_bass_kernel_spmd

F32 = mybir.dt.float32
BF16 = mybir.dt.bfloat16
I32 = mybir.dt.int32
AF = mybir.ActivationFunctionType
ALU = mybir.AluOpType
AX = mybir.AxisListType

D = 1024
S = 4096
L = 2
NT = S // 128
NB = S // 512
DFF = 2816
NFC = DFF // 128
EPS = 1e-6
THETA = 500000.0
NEG = -30000.0

O_NQ = 0
O_NKV = 512
O_NG = 1280
O_FQ = 1304
O_FK = 1816
O_FV = 2328
O_FF = 2840
O_CQ = 2848
O_CKV = 3232
O_KR = 3488
O_MG = 3520
N_IN = 6592


def _swap_cols_nsa(base, nheads):
    idx = []
    for h in range(nheads):
        b = base + 64 * h
        idx += list(range(b + 8, b + 16)) + list(range(b, b + 8)) + list(range(b + 16, b + 64))
    return idx


def _win_ext_cols():
    groups = []
    for p in range(4):
        a = list(range(O_NQ + 128 * p, O_NQ + 128 * p + 128))
        b = _swap_cols_nsa(O_NQ + 128 * p, 2)
        groups.append(a + b)
    kc = list(range(O_NKV, O_NKV + 128))
    vc = list(range(O_NKV + 128, O_NKV + 256))
    ks = list(range(O_NKV + 256, O_NKV + 384))
    ksb = _swap_cols_nsa(O_NKV + 256, 2)
    groups.append(kc + vc + ks + ksb)
    kw = list(range(O_NKV + 512, O_NKV + 640))
    kwb = _swap_cols_nsa(O_NKV + 512, 2)
    kr = list(range(O_KR, O_KR + 32))
    krb = list(range(O_KR + 16, O_KR + 32)) + list(range(O_KR, O_KR + 16))
    ff = list(range(O_FF, O_FF + 8))
    groups.append(kw + kwb + kr + krb + ff)
    groups.append(list(range(O_FQ, O_FQ + 512)))
    groups.append(list(range(O_FK, O_FK + 512)))
    for i in range(6):
        groups.append(list(range(O_MG + 512 * i, O_MG + 512 * i + 512)))
    vs = list(range(O_NKV + 384, O_NKV + 512))
    vw = list(range(O_NKV + 640, O_NKV + 768))
    ng = list(range(O_NG, O_NG + 24))
    groups.append(vs + vw + ng)
    groups.append(list(range(O_FV, O_FV + 512)))
    groups.append(list(range(O_CQ, O_CQ + 384)))
    groups.append(list(range(O_CKV, O_CKV + 256)))
    return groups


WIN_GROUPS = _win_ext_cols()
WIN_GOFF = np.cumsum([0] + [len(g) for g in WIN_GROUPS]).tolist()
NEXT = WIN_GOFF[-1]

V_MIXN, V_FFNN, V_QN, V_KVN, V_CW0, V_CW1, V_CW2, V_CB, V_BF, V_PEK, V_PEV = 0, 8, 16, 19, 21, 43, 65, 87, 109, 110, 142
NVEC = 174


def _box(ap):
    t = ap.tensor
    shp = t.shape
    rowsize = 1
    for s in shp[1:]:
        rowsize *= s
    off = int(ap.offset)
    r0 = off // rowsize
    c0 = off % rowsize
    rext = 0
    cext = 0
    for step, cnt in ap.ap:
        if cnt <= 1 or step == 0:
            continue
        if step >= rowsize:
            rext += (step // rowsize) * (cnt - 1)
        else:
            cext += step * (cnt - 1)
    c1 = c0 + cext + 1
    if type(t).__name__ == "PSumTensorHandle":
        return (t.name, 0, 128, -2, -1)
    if c1 > rowsize:
        rext += (c1 - 1) // rowsize
        c0, c1 = 0, rowsize
    return (t.name, r0, r0 + rext + 1, c0, c1)


def _ovl(a, b):
    return a[1] < b[2] and b[1] < a[2] and a[3] < b[4] and b[3] < a[4]


def _contains(a, b):
    return a[1] <= b[1] and b[2] <= a[2] and a[3] <= b[3] and b[4] <= a[4]


class _Op:
    __slots__ = ("eng", "fn", "rd", "wr", "dma", "deps", "sig", "sem", "val", "barrier")


class Prog:
    K_DMA = 16

    def __init__(self, nc, es):
        self.nc = nc
        self.engs = {"pe": nc.tensor, "dve": nc.vector, "act": nc.scalar, "pool": nc.gpsimd, "sp": nc.sync}
        self.sems = {k: es.enter_context(nc.semaphore("s_" + k)) for k in ("pe", "dve", "act", "pool")}
        self.dsems = {q: [es.enter_context(nc.semaphore("d_%s%d" % (q, i))) for i in range(self.K_DMA)]
                      for q in ("sp", "act", "pool")}
        self.cnt = {k: 0 for k in self.sems}
        self.dcnt = {q: 0 for q in self.dsems}
        self.seen = {k: {} for k in self.engs}
        self.last = {}
        self.ops = []
        self.n_inst = 0

    def add(self, eng, fn, reads, writes, dma=False):
        o = _Op()
        o.eng, o.fn, o.dma = eng, fn, dma
        o.rd = [_box(a) for a in reads if a is not None and not isinstance(a, (int, float))]
        o.wr = [_box(a) for a in writes if a is not None]
        o.deps, o.sig, o.sem, o.val, o.barrier = None, False, None, 0, False
        self.ops.append(o)
        return o

    def barrier(self):
        for e in self.engs:
            o = _Op()
            o.eng, o.fn, o.dma, o.rd, o.wr = e, None, False, [], []
            o.deps, o.sig, o.sem, o.val, o.barrier = None, False, None, 0, True
            self.ops.append(o)

    def mm(self, out, lhsT, rhs, start=True, stop=True):
        self.add("pe", lambda e: e.matmul(out, lhsT, rhs, start=start, stop=stop), [lhsT, rhs], [out])

    def tr(self, out, in_, ident):
        self.add("pe", lambda e: e.transpose(out, in_, ident), [in_, ident], [out])

    def act(self, out, in_, func, bias=None, scale=None, accum_out=None):
        kw = {}
        if bias is not None:
            kw["bias"] = bias
        if scale is not None:
            kw["scale"] = scale
        if accum_out is not None:
            kw["accum_out"] = accum_out
        self.add("act", lambda e: e.activation(out, in_, func, **kw), [in_, bias, scale], [out, accum_out])

    def ts(self, eng, out, in0, s1, s2=None, op0=ALU.mult, op1=None):
        if op1 is None:
            fn = lambda e: e.tensor_scalar(out, in0, s1, s2, op0)
        else:
            fn = lambda e: e.tensor_scalar(out, in0, s1, s2, op0, op1)
        self.add(eng, fn, [in0, s1, s2], [out])

    def tt(self, eng, out, in0, in1, op):
        self.add(eng, lambda e: e.tensor_tensor(out, in0, in1, op), [in0, in1], [out])

    def stt(self, eng, out, in0, scalar, in1, op0, op1):
        self.add(eng, lambda e: e.scalar_tensor_tensor(out, in0, scalar, in1, op0, op1), [in0, scalar, in1], [out])

    def copy(self, eng, out, in_):
        if eng == "act":
            self.add("act", lambda e: e.copy(out, in_), [in_], [out])
        else:
            self.add(eng, lambda e: e.tensor_copy(out, in_), [in_], [out])

    def amul(self, out, in_, m):
        self.add("act", lambda e: e.mul(out, in_, m), [in_, m], [out])

    def memset(self, eng, out, val):
        self.add(eng, lambda e: e.memset(out, val), [], [out])

    def recip(self, out, in_):
        self.add("dve", lambda e: e.reciprocal(out, in_), [in_], [out])

    def dma(self, q, out, in_, **kw):
        self.add(q, lambda e: e.dma_start(out=out, in_=in_, **kw), [in_], [out], dma=True)

    @staticmethod
    def _need(op, d, raw):
        if d.dma or op.dma:
            return True
        if d.eng != op.eng:
            return True
        if op.eng == "pe":
            return False
        return True

    def flush(self):
        body = self.ops
        self.ops = []
        self.barrier()
        ops = self.ops + body
        self.ops = []
        recs = {}
        for i, o in enumerate(ops):
            if o.barrier:
                o.deps = {}
                recs = {}
                continue
            deps = {}
            for b in o.rd:
                tb = recs.get(b[0])
                if tb:
                    for bb, rec in tb.items():
                        if _ovl(b, bb):
                            if rec[0] is not None:
                                deps[rec[0]] = True
                            if b[4] == -1:
                                for r in rec[1].values():
                                    if r.eng != o.eng:
                                        deps.setdefault(r, False)
            for b in o.wr:
                tb = recs.get(b[0])
                if tb:
                    for bb, rec in tb.items():
                        if _ovl(b, bb):
                            if rec[0] is not None:
                                deps.setdefault(rec[0], False)
                            for r in rec[1].values():
                                deps.setdefault(r, False)
            deps.pop(o, None)
            o.deps = deps
            for b in o.rd:
                tb = recs.setdefault(b[0], {})
                rec = tb.get(b)
                if rec is None:
                    rec = tb[b] = [None, {}]
                rec[1][(o.eng, i) if o.dma else o.eng] = o
            for b in o.wr:
                tb = recs.setdefault(b[0], {})
                for bb in [bb for bb in tb if _contains(b, bb)]:
                    del tb[bb]
                tb[b] = [o, {}]
        for o in ops:
            for d, raw in o.deps.items():
                if self._need(o, d, raw):
                    d.sig = True
        lastop = {}
        for o in ops:
            if not o.dma and not o.barrier and o.eng in self.sems:
                lastop[o.eng] = o
        for o in lastop.values():
            o.sig = True
        for o in ops:
            e = self.engs[o.eng]
            if o.barrier:
                for nm, (sem, val) in list(self.last.items()):
                    if self.seen[o.eng].get(nm, 0) < val:
                        e.wait_ge(sem, val)
                        self.seen[o.eng][nm] = val
                        self.n_inst += 1
                continue
            need = {}
            for d, raw in o.deps.items():
                if self._need(o, d, raw):
                    nm = d.sem.name
                    if need.get(nm, (None, 0))[1] < d.val:
                        need[nm] = (d.sem, d.val)
            for nm, (sem, val) in need.items():
                if self.seen[o.eng].get(nm, 0) < val:
                    e.wait_ge(sem, val)
                    self.seen[o.eng][nm] = val
                    self.n_inst += 1
            if o.dma:
                i = self.dcnt[o.eng]
                self.dcnt[o.eng] = i + 1
                o.sem = self.dsems[o.eng][i % self.K_DMA]
                o.val = 16 * (i // self.K_DMA + 1)
                if o.val > 16 and self.seen[o.eng].get(o.sem.name, 0) < o.val - 16:
                    e.wait_ge(o.sem, o.val - 16)
                    self.seen[o.eng][o.sem.name] = o.val - 16
                    self.n_inst += 1
            elif o.sig:
                self.cnt[o.eng] += 1
                o.sem = self.sems[o.eng]
                o.val = self.cnt[o.eng]
            ins = o.fn(e)
            self.n_inst += 1
            if o.dma:
                ins.then_inc(o.sem, 16)
                self.last[o.sem.name] = (o.sem, o.val)
            elif o.sig:
                ins.then_inc(o.sem, 1)
                self.last[o.sem.name] = (o.sem, o.val)

    def finish(self):
        self.flush()


class Ctx:
    pass


def _dram(nc, name, shape, dt, kind="Internal"):
    return nc.dram_tensor(name, list(shape), dt, kind=kind).ap()


def build_program(dbg=None):
    dbg = dbg or {}
    outs = set(dbg.get("outs", ()))
    nc = bass.Bass("TRN2", target_bir_lowering=False)
    C = Ctx()
    C.nc = nc
    C.dbg = dbg

    def ext_in(name, shape, dt=F32):
        return _dram(nc, name, shape, dt, "ExternalInput")

    def scr(name, shape, dt=BF16):
        return _dram(nc, name, shape, dt, "ExternalOutput" if name in outs else "Internal")

    C.x = ext_in("x", [S, D])
    C.pos = ext_in("pos", [1, S], I32)
    C.wext = ext_in("wext", [L, D, NEXT])
    C.vecs = ext_in("vecs", [L, 128, NVEC])
    C.rotc = ext_in("rotc", [128, 4])
    C.identf = ext_in("identf", [128, 128])
    C.wbr = ext_in("wbr", [L, 3, 512, D])
    C.wout = ext_in("wout", [L, D, D])
    C.wupx = ext_in("wupx", [L, D, 2 * DFF])
    C.wdown = ext_in("wdown", [L, DFF, D])
    C.fin = ext_in("fin", [128, D])
    C.maskc = ext_in("maskc", [128, 8, 512], BF16)
    C.wuqx = ext_in("wuqx", [L, 384, 1536])
    C.cw1k = ext_in("cw1k", [L, 2048, 256])
    C.cw1v = ext_in("cw1v", [L, 2048, 256])
    C.cw2k = ext_in("cw2k", [L, 256, 64])
    C.cw2v = ext_in("cw2v", [L, 256, 64])
    C.ovl = ext_in("ovl", [256, 64])
    C.cmpm = ext_in("cmpm", [128, 2, S], BF16)
    C.forc = ext_in("forc", [2, 128, NT * 64])
    C.esel = ext_in("esel", [64, S], BF16)
    C.ocmpT = scr("ocmpT", [512, S], F32)
    C.ngT = scr("ngT", [24, S], F32)
    C.selTd = scr("selTd", [2, 64, S])
    C.wukvx = ext_in("wukvx", [L, 256, 1024])
    C.y = _dram(nc, "y", [S, D], F32, "ExternalOutput")
    C.oT = [scr("oT%d" % i, [512, S]) for i in range(3)]
    C.yT = scr("yT", [DFF, S])
    C.xmid = scr("xmid", [S, D], F32)
    C.tabs = scr("tabs", [4, 128, S])
    C.xres = scr("xres", [S, D], F32)
    C.nq = scr("nq", [512, S])
    C.nqr = scr("nqr", [512, S])
    C.nkc = scr("nkc", [128, S])
    C.nvc = scr("nvc", [128, S])
    C.nks = scr("nks", [128, S])
    C.nkw = scr("nkw", [128, S])
    C.nvs = scr("nvs", [S, 128])
    C.nvw = scr("nvw", [S, 128])
    C.ng = scr("ng", [S, 24], F32)
    C.fq = scr("fq", [8, 68, S])
    C.fk = scr("fk", [8, 68, S])
    C.fv = scr("fv", [S, 512])
    C.flog = scr("flog", [8, S], F32)
    C.mcq = scr("mcq", [384, S])
    C.mckv = scr("mckv", [256, S])
    C.mkr = scr("mkr", [32, S])
    C.gm = scr("gm", [3072, S])

    with contextlib.ExitStack() as es:
        P = Prog(nc, es)
        C.P = P
        phase_setup(C)
        for l in range(L if dbg.get("stop") != "setup" else 0):
            xin = C.x if l == 0 else C.xres
            ph = dbg.get("phases", "abcdef")
            if "a" in ph:
                phase_a(C, l, xin)
            if dbg.get("stop") in ("a", "norm"):
                break
            if "b" in ph:
                phase_b(C, l)
            if "c" in ph:
                phase_c(C, l)
            if "d" in ph:
                phase_d(C, l)
            if "e" in ph:
                phase_e(C, l, xin, C.xmid)
            if "f" in ph:
                phase_f(C, l, C.xmid, C.y if l == L - 1 else C.xres, l == L - 1)
            if dbg.get("stop") == "l0":
                break
        P.finish()
    return nc


def phase_setup(C):
    nc, P = C.nc, C.P
    with contextlib.ExitStack() as es:
        sb = lambda n, s, d: es.enter_context(nc.sbuf_tensor("su_" + n, s, d))
        posi = sb("posi", [128, S], I32)
        posf = sb("posf", [128, S], F32)
        ang = sb("ang", [128, S], F32)
        u = sb("u", [128, S], F32)
        tb = sb("tb", [128, S], BF16)
        rc = sb("rc", [128, 4], F32)
        P.dma("sp", posi[:], C.pos.broadcast_to([128, S]))
        P.dma("sp", rc[:], C.rotc)
        P.copy("dve", posf[:], posi[:])
        two_pi = 2.0 * math.pi
        ki = sb("ki", [128, S], I32)
        kf = sb("kf", [128, S], F32)

        def sin_of(dst, a):
            P.ts("dve", u[:], a, 1.0 / two_pi, None, ALU.mult)
            P.copy("dve", ki[:], u[:])
            P.copy("dve", kf[:], ki[:])
            P.stt("dve", u[:], kf[:], -two_pi, a, ALU.mult, ALU.add)
            P.ts("dve", kf[:], u[:], math.pi, None, ALU.is_gt)
            P.stt("dve", u[:], kf[:], -two_pi, u[:], ALU.mult, ALU.add)
            P.ts("dve", kf[:], u[:], -math.pi, None, ALU.is_lt)
            P.stt("dve", u[:], kf[:], two_pi, u[:], ALU.mult, ALU.add)
            P.act(dst, u[:], AF.Sin)

        for ti, col in ((0, 0), (1, 2)):
            P.ts("dve", ang[:], posf[:], rc[:, col:col + 1], None, ALU.mult)
            P.ts("dve", posi[:].bitcast(F32), ang[:], 0.5 * math.pi, None, ALU.add)
            sin_of(tb[:], posi[:].bitcast(F32))
            P.dma("sp", C.tabs[2 * ti], tb[:])
            sin_of(kf[:], ang[:])
            P.ts("dve", tb[:], kf[:], rc[:, col + 1:col + 2], None, ALU.mult)
            P.dma("sp", C.tabs[2 * ti + 1], tb[:])
        P.flush()


def norm_T(C, l, xin, hT, gcol, es_name):
    nc, P = C.nc, C.P
    with contextlib.ExitStack() as es:
        sb = lambda n, s, d: es.enter_context(nc.sbuf_tensor(es_name + n, s, d))
        xt = [sb("xt%d" % i, [128, D], F32) for i in range(3)]
        junk = sb("junk", [128, D], BF16)
        xs = [sb("xs%d" % i, [128, D], BF16) for i in range(3)]
        ss = sb("ss", [128, NT], F32)
        sq = sb("sq", [128, NT], F32)
        rstd = sb("rstd", [128, NT], F32)
        pst = [es.enter_context(nc.psum_tensor(es_name + "pst%d" % i, [128, D], BF16)) for i in range(2)]
        def stage1(i):
            x_t = xt[i % 3]
            P.dma("sp", x_t[:], xin[i * 128:(i + 1) * 128, :])
            P.act(junk[:], x_t[:], AF.Square, accum_out=ss[:, i:i + 1])
            P.act(sq[:, i:i + 1], ss[:, i:i + 1], AF.Sqrt, bias=C.epsb[:], scale=1.0 / D)
            P.recip(rstd[:, i:i + 1], sq[:, i:i + 1])
            x_s = xs[i % 3]
            P.ts("dve", x_s[:], x_t[:], rstd[:, i:i + 1], None, ALU.mult)
            ps = pst[i % 2]
            for c in range(8):
                P.tr(ps[:, c * 128:(c + 1) * 128], x_s[:, c * 128:(c + 1) * 128], C.identb[:])

        def stage2(i):
            ps = pst[i % 2]
            for c in range(8):
                o = hT[:, c, i * 128:(i + 1) * 128]
                g = C.vp[:, gcol + c:gcol + c + 1]
                if i % 2 == 0:
                    P.amul(o, ps[:, c * 128:(c + 1) * 128], g)
                else:
                    P.ts("dve", o, ps[:, c * 128:(c + 1) * 128], g, None, ALU.mult)

        for i in range(NT + 1):
            if i < NT:
                stage1(i)
            if i >= 1:
                stage2(i - 1)
        P.flush()


def phase_a(C, l, xin):
    nc, P = C.nc, C.P
    with contextlib.ExitStack() as es:
        sb = lambda n, s, d: es.enter_context(nc.sbuf_tensor("a%d_" % l + n, s, d))
        ps = lambda n, s, d: es.enter_context(nc.psum_tensor("a%d_" % l + n, s, d))
        C.vp = sb("vp", [128, NVEC], F32)
        C.epsb = sb("epsb", [128, 1], F32)
        identf = sb("identf", [128, 128], F32)
        C.identb = sb("identb", [128, 128], BF16)
        hT = sb("hT", [128, 8, S], BF16)
        P.dma("sp", C.vp[:], C.vecs[l])
        P.dma("sp", identf[:], C.identf)
        P.memset("dve", C.epsb[:], EPS)
        P.copy("dve", C.identb[:], identf[:])
        norm_T(C, l, xin, hT, V_MIXN, "a%dn_" % l)
        if C.dbg.get("stop") == "norm":
            P.flush()
            return

        cosn = sb("cosn", [128, S], BF16)
        sinn = sb("sinn", [128, S], BF16)
        cosm = sb("cosm", [32, S], BF16)
        sinm = sb("sinm", [32, S], BF16)
        P.dma("sp", cosn[:], C.tabs[0])
        P.dma("sp", sinn[:], C.tabs[1])
        P.dma("sp", cosm[:], C.tabs[2, 0:32, :])
        P.dma("sp", sinm[:], C.tabs[3, 0:32, :])

        wst = [sb("wst%d" % i, [128, 8, 512], F32) for i in range(2)]
        wbf = [sb("wbf%d" % i, [128, 8, 512], BF16) for i in range(2)]
        pA = [ps("pA%d" % i, [128, 512], F32) for i in range(3)]
        pB = [ps("pB%d" % i, [128, 512], F32) for i in range(2)]
        pT = ps("pT", [128, 1024], BF16)
        ob = [sb("ob%d" % i, [128, 512], BF16) for i in range(4)]
        of = [sb("of%d" % i, [128, 512], F32) for i in range(2)]
        t1 = [sb("t1%d" % i, [128, 512], F32) for i in range(2)]
        t2 = [sb("t2%d" % i, [128, 512], F32) for i in range(2)]
        mss = sb("mss", [128, 4], F32)
        mo = [sb("mo%d" % i, [128, 512], BF16) for i in range(4)]
        cnt = {"a": 0, "b": 0, "o": 0, "f": 0, "t": 0, "e": 0}

        loaded = {}

        def load_group(gi):
            if gi not in loaded:
                loaded[gi] = load_group_(gi)
            if gi + 1 < len(WIN_GROUPS) and gi + 1 not in loaded:
                loaded[gi + 1] = load_group_(gi + 1)
            return loaded[gi]

        def load_group_(gi):
            n = len(WIN_GROUPS[gi])
            w_s, w_b = wst[gi % 2], wbf[gi % 2]
            src = C.wext[l, :, WIN_GOFF[gi]:WIN_GOFF[gi] + n].rearrange("(c p) n -> p c n", p=128)
            P.dma("sp", w_s[:, :, 0:n], src)
            for c in range(8):
                P.copy("act", w_b[:, c, 0:n], w_s[:, c, 0:n])
            return w_b

        def fm_mm(w_b, c0, m, b, pt):
            for c in range(8):
                P.mm(pt[0:m, :], w_b[:, c, c0:c0 + m], hT[:, c, b * 512:(b + 1) * 512], start=(c == 0), stop=(c == 7))

        def evac_eng():
            cnt["e"] += 1
            return "act" if cnt["e"] % 2 else "dve"

        def fm_copy(w_b, c0, m, dst_fn, func=None, dt=BF16):
            for b in range(NB):
                pt = pA[cnt["a"] % 3]
                cnt["a"] += 1
                fm_mm(w_b, c0, m, b, pt)
                if dt == BF16:
                    o = ob[cnt["o"] % 4]
                    cnt["o"] += 1
                else:
                    o = of[cnt["f"] % 2]
                    cnt["f"] += 1
                if func is not None:
                    P.act(o[0:m, :], pt[0:m, :], func)
                else:
                    P.copy(evac_eng(), o[0:m, :], pt[0:m, :])
                for (dst, r0, r1) in dst_fn(b):
                    P.dma("sp", dst, o[r0:r1, :])

        def fm_rot(w_b, ca, cb, m, cost, sint, dst_fn, plain_fn=None):
            for b in range(NB):
                pa = pA[cnt["a"] % 3]
                cnt["a"] += 1
                pb = pB[cnt["b"] % 2]
                cnt["b"] += 1
                fm_mm(w_b, ca, m, b, pa)
                fm_mm(w_b, cb, m, b, pb)
                bs = slice(b * 512, (b + 1) * 512)
                a1, a2 = t1[cnt["t"] % 2], t2[cnt["t"] % 2]
                cnt["t"] += 1
                P.tt("dve", a1[0:m, :], pb[0:m, :], sint[0:m, bs], ALU.mult)
                P.tt("dve", a2[0:m, :], pa[0:m, :], cost[0:m, bs], ALU.mult)
                o = ob[cnt["o"] % 4]
                cnt["o"] += 1
                P.tt("pool", o[0:m, :], a1[0:m, :], a2[0:m, :], ALU.add)
                for (dst, r0, r1) in dst_fn(b):
                    P.dma("sp", dst, o[r0:r1, :])
                if plain_fn is not None:
                    o2 = ob[cnt["o"] % 4]
                    cnt["o"] += 1
                    P.copy("act", o2[0:m, :], pa[0:m, :])
                    for (dst, r0, r1) in plain_fn(b):
                        P.dma("sp", dst, o2[r0:r1, :])

        def rows(dst, r0, n=128):
            return lambda b: [(dst[r0:r0 + n, b * 512:(b + 1) * 512], 0, n)]

        gi = 0
        for p in range(4):
            w_b = load_group(gi)
            fm_rot(w_b, 0, 128, 128, cosn, sinn, rows(C.nqr, 128 * p), rows(C.nq, 128 * p))
            gi += 1
        w_b = load_group(gi)
        fm_copy(w_b, 0, 128, rows(C.nkc, 0))
        fm_copy(w_b, 128, 128, rows(C.nvc, 0))
        fm_rot(w_b, 256, 384, 128, cosn, sinn, rows(C.nks, 0))
        gi += 1
        w_b = load_group(gi)
        fm_rot(w_b, 0, 128, 128, cosn, sinn, rows(C.nkw, 0))
        fm_rot(w_b, 256, 288, 32, cosm, sinm, rows(C.mkr, 0, 32))
        fm_copy(w_b, 320, 8, lambda b: [(C.flog[:, b * 512:(b + 1) * 512], 0, 8)], dt=F32)
        gi += 1
        for dst in (C.fq, C.fk):
            w_b = load_group(gi)
            for p in range(4):
                fm_copy(w_b, 128 * p, 128,
                        (lambda p, dst: lambda b: [(dst[2 * p, 0:64, b * 512:(b + 1) * 512], 0, 64),
                                                  (dst[2 * p + 1, 0:64, b * 512:(b + 1) * 512], 64, 128)])(p, dst))
            gi += 1
        for i in range(6):
            w_b = load_group(gi)
            for p in range(4):
                fm_copy(w_b, 128 * p, 128, rows(C.gm, 512 * i + 128 * p), func=AF.Sigmoid)
            gi += 1

        def tm_mm(w_b, c0, n, i, pt):
            for c in range(8):
                P.mm(pt[:, 0:n], hT[:, c, i * 128:(i + 1) * 128], w_b[:, c, c0:c0 + n], start=(c == 0), stop=(c == 7))

        w_b = load_group(gi)
        fm_copy(w_b, 256, 24, lambda b: [(C.ngT[:, b * 512:(b + 1) * 512], 0, 24)], func=AF.Sigmoid, dt=F32)
        for i in range(NT):
            pt = pA[cnt["a"] % 3]
            cnt["a"] += 1
            tm_mm(w_b, 0, 280, i, pt)
            o = ob[cnt["o"] % 4]
            cnt["o"] += 1
            P.copy(evac_eng(), o[:, 0:256], pt[:, 0:256])
            o2 = of[cnt["f"] % 2]
            cnt["f"] += 1
            P.act(o2[:, 0:24], pt[:, 256:280], AF.Sigmoid)
            tsl = slice(i * 128, (i + 1) * 128)
            P.dma("sp", C.nvs[tsl, :], o[:, 0:128])
            P.dma("sp", C.nvw[tsl, :], o[:, 128:256])
            P.dma("sp", C.ng[tsl, :], o2[:, 0:24])
        gi += 1
        w_b = load_group(gi)
        for i in range(NT):
            pt = pA[cnt["a"] % 3]
            cnt["a"] += 1
            tm_mm(w_b, 0, 512, i, pt)
            o = ob[cnt["o"] % 4]
            cnt["o"] += 1
            P.copy(evac_eng(), o[:, :], pt[:, :])
            P.dma("sp", C.fv[i * 128:(i + 1) * 128, :], o[:, :])
        gi += 1
        for (n, dst) in ((384, C.mcq), (256, C.mckv)):
            w_b = load_group(gi)
            nch = n // 128
            pend = []

            def tail(i, o, n=n, nch=nch, dst=dst):
                for c in range(nch):
                    P.tr(pT[:, c * 128:(c + 1) * 128], o[:, c * 128:(c + 1) * 128], C.identb[:])
                o3 = ob[cnt["o"] % 4]
                cnt["o"] += 1
                P.copy("act", o3[:, 0:n], pT[:, 0:n])
                for c in range(nch):
                    P.dma("sp", dst[c * 128:(c + 1) * 128, i * 128:(i + 1) * 128], o3[:, c * 128:(c + 1) * 128])

            for i in range(NT):
                pt = pA[cnt["a"] % 3]
                cnt["a"] += 1
                tm_mm(w_b, 0, n, i, pt)
                o2 = of[cnt["f"] % 2]
                cnt["f"] += 1
                j = i % 4
                P.act(o2[:, 0:n], pt[:, 0:n], AF.Square, accum_out=mss[:, j:j + 1])
                P.act(mss[:, j:j + 1], mss[:, j:j + 1], AF.Sqrt, bias=C.epsb[:], scale=1.0 / n)
                P.recip(mss[:, j:j + 1], mss[:, j:j + 1])
                o = mo[i % 4]
                P.ts("dve", o[:, 0:n], pt[:, 0:n], mss[:, j:j + 1], None, ALU.mult)
                pend.append((i, o))
                if len(pend) > 2:
                    tail(*pend.pop(0))
            while pend:
                tail(*pend.pop(0))
            gi += 1
        P.flush()


class Pipe:
    def __init__(self, la):
        self.LA = la
        self.fifo = []
        self.tick = 0

    def pair(self, qk_fn, pv_fn):
        qk_fn()
        self.tick += 1
        self.fifo.append((self.tick + self.LA, pv_fn))
        self.pump()

    def defer(self, fn, delay):
        self.fifo.append((self.tick + delay, fn))

    def pump(self):
        i = 0
        while i < len(self.fifo):
            if self.fifo[i][0] <= self.tick:
                self.fifo.pop(i)[1]()
            else:
                i += 1

    def drain(self):
        while self.fifo:
            self.tick += 1
            self.pump()


class AttnBufs(Pipe):
    def __init__(self, C, es, tag, npo=2):
        Pipe.__init__(self, 2)
        self.npo = npo
        nc = C.nc
        sb = lambda n, s, d: es.enter_context(nc.sbuf_tensor(tag + n, s, d))
        ps = lambda n, s, d: es.enter_context(nc.psum_tensor(tag + n, s, d))
        self.C = C
        self.pss = [ps("pss%d" % i, [128, 512], F32) for i in range(3)]
        self.po = [ps("po%d" % i, [128, 512], F32) for i in range(npo)]
        self.pbc = ps("pbc", [128, 512], F32)
        self.nfill = C.dbg.get("fill", 0)
        self.pfill = ps("pfill", [128, 512], F32) if self.nfill else None
        self.pt = [sb("pt%d" % i, [128, 512], BF16) for i in range(3)]
        self.osb = [sb("osb%d" % i, [65, 512], F32) for i in range(8)]
        self.obf = [sb("obf%d" % i, [64, 512], BF16) for i in range(2)]
        self.sel = sb("sel65", [65, 64], F32)
        C.P.memset("pool", self.sel[:], 0.0)
        C.P.memset("pool", self.sel[64:65, :], 1.0)
        self.masks = sb("masks", [128, 8, 512], BF16)
        C.P.dma("sp", self.masks[:], C.maskc)
        self.ns = 0
        self.nacc = 0
        self.nob = 0

    def score(self, kT_tile, qT_blk, c0, c1, scale, extra=()):
        P = self.C.P
        p_s, p_t = self.pss[self.ns % 3], self.pt[self.ns % 3]
        self.ns += 1
        P.mm(p_s[:, c0:c1], kT_tile, qT_blk[:, c0:c1], start=True, stop=(len(extra) == 0))
        for i, (lt, rh, e0, e1) in enumerate(extra):
            P.mm(p_s[:, e0:e1], lt, rh[:, e0:e1], start=False, stop=(i == len(extra) - 1))
        P.act(p_t[:, c0:c1], p_s[:, c0:c1], AF.Exp, scale=scale)
        for _ in range(self.nfill):
            P.mm(self.pfill[:, c0:c1], kT_tile, qT_blk[:, c0:c1], start=True, stop=True)
        return p_t

    def new_acc(self):
        k = self.nacc
        self.nacc += 1
        return self.po[k % self.npo], self.osb[k % 8]


def attn_finish(C, B, po, osb, dst, q, gate=None, addT=None):
    P = C.P
    P.copy("dve", osb[:], po[0:65, :])
    P.recip(osb[64:65, :], osb[64:65, :])
    if gate is not None:
        P.tt("dve", osb[64:65, :], osb[64:65, :], gate, ALU.mult)

    def e2():
        P.mm(B.pbc[0:64, :], B.sel[:], osb[:], start=True, stop=True)
        if dst is None:
            P.tt("dve", osb[0:64, :], osb[0:64, :], B.pbc[0:64, :], ALU.mult)
            return
        ob = B.obf[B.nob % 2]
        B.nob += 1
        if addT is None:
            P.tt("dve", ob[:], osb[0:64, :], B.pbc[0:64, :], ALU.mult)
        else:
            P.tt("dve", osb[0:64, :], osb[0:64, :], B.pbc[0:64, :], ALU.mult)
            for a in addT[:-1]:
                P.tt("pool", osb[0:64, :], osb[0:64, :], a, ALU.add)
            P.tt("pool", ob[:], osb[0:64, :], addT[-1], ALU.add)
        P.dma("sp", dst[:, q * 512:(q + 1) * 512], ob[:])
    B.defer(e2, 14)


def dense_causal(C, B, qT, kT, kr, vaug, scale, dst_rows):
    P = C.P
    for q in range(NB):
        po, osb = B.new_acc()
        nk = 4 * q + 4
        for kt in range(nk):
            m = kt - 4 * q
            c0 = 128 * max(m, 0)

            def qk(kt=kt, m=m, c0=c0, q=q):
                extra = [(C.identb[:], B.masks[:, m, :], 128 * m, 128 * m + 128)] if m >= 0 else []
                return B.score(kT[0:kr, kt * 128:(kt + 1) * 128], qT[0:kr, q * 512:(q + 1) * 512], c0, 512, scale, extra)
            holder = {}

            def qk_fn(qk=qk, holder=holder):
                holder["pt"] = qk()

            def pv_fn(kt=kt, c0=c0, q=q, nk=nk, holder=holder, po=po, osb=osb):
                P.mm(po[0:vaug.shape[-1], c0:512], vaug[:, kt, :], holder["pt"][:, c0:512], start=(kt == 0), stop=(kt == nk - 1))
                if kt == nk - 1:
                    attn_finish(C, B, po, osb, dst_rows, q)
            B.pair(qk_fn, pv_fn)


def fox_decay(C, l, sb, vp):
    nc, P = C.nc, C.P
    a = sb("a", [8, S], F32)
    b = sb("b", [8, S], F32)
    c = sb("c", [8, S], F32)
    hi = sb("hi", [8, S], BF16)
    lo = sb("lo", [8, S], BF16)
    on = sb("on", [8, S], BF16)
    P.dma("sp", a[:], C.flog)
    P.ts("dve", a[:], a[:], vp[0:8, V_BF:V_BF + 1], None, ALU.add)
    P.act(b[:], a[:], AF.Exp, scale=-1.0)
    P.act(a[:], b[:], AF.Ln, bias=1.0, scale=1.0)
    P.memset("dve", b[:], 1.0)
    P.add("dve", lambda e: e.tensor_tensor_scan(c[:], b[:], a[:], 0.0, ALU.mult, ALU.add), [a[:], b[:]], [c[:]])
    P.ts("dve", c[:], c[:], 8.0, None, ALU.mult)
    P.copy("dve", hi[:], c[:])
    P.copy("dve", b[:], hi[:])
    P.tt("dve", a[:], c[:], b[:], ALU.subtract)
    P.copy("dve", lo[:], a[:])
    P.memset("dve", on[:], 1.0)
    P.dma("sp", C.fk[:, 66, :], hi[:])
    P.dma("sp", C.fk[:, 67, :], lo[:])
    P.dma("sp", C.fk[:, 64, :], on[:])
    P.dma("sp", C.fk[:, 65, :], on[:])
    P.dma("sp", C.fq[:, 66, :], on[:])
    P.dma("sp", C.fq[:, 67, :], on[:])
    P.ts("dve", hi[:], hi[:], -1.0, None, ALU.mult)
    P.ts("dve", lo[:], lo[:], -1.0, None, ALU.mult)
    P.dma("sp", C.fq[:, 64, :], hi[:])
    P.dma("sp", C.fq[:, 65, :], lo[:])


def phase_c(C, l):
    nc, P = C.nc, C.P
    with contextlib.ExitStack() as es:
        sb = lambda n, s, d: es.enter_context(nc.sbuf_tensor("c%d_" % l + n, s, d))
        identf = sb("identf", [128, 128], F32)
        C.identb = sb("identb", [128, 128], BF16)
        P.dma("sp", identf[:], C.identf)
        P.copy("dve", C.identb[:], identf[:])
        B = AttnBufs(C, es, "c%d_" % l)
        qT = [sb("qT%d" % i, [128, S], BF16) for i in range(2)]
        kT = [sb("kT%d" % i, [128, S], BF16) for i in range(2)]
        va = [sb("va%d" % i, [128, NT, 128], BF16) for i in range(2)]
        for i in range(2):
            P.memset("pool", va[i][:, :, 64:128], 0.0)
            P.memset("pool", va[i][:, :, 64:65], 1.0)
            P.memset("pool", qT[i][64:128, :], 0.0)
            P.memset("pool", kT[i][64:128, :], 0.0)
        def loadh(h):
            k = h % 2
            P.dma("sp", qT[k][0:68, :], C.fq[h])
            P.dma("sp", kT[k][0:68, :], C.fk[h])
            P.dma("sp", va[k][:, :, 0:64], C.fv[:, h * 64:(h + 1) * 64].rearrange("(n p) d -> p n d", p=128))
        loadh(0)
        for h in range(8):
            k = h % 2
            if h + 1 < 8:
                B.drain()
                loadh(h + 1)
            dense_causal(C, B, qT[k], kT[k], 128, va[k], 0.125, C.oT[1][h * 64:(h + 1) * 64, :])
        B.drain()
        P.flush()


def phase_d(C, l):
    nc, P = C.nc, C.P
    with contextlib.ExitStack() as es:
        sb = lambda n, s, d: es.enter_context(nc.sbuf_tensor("d%d_" % l + n, s, d))
        ps = lambda n, s, d: es.enter_context(nc.psum_tensor("d%d_" % l + n, s, d))
        vp = sb("vp", [128, NVEC], F32)
        P.dma("sp", vp[:], C.vecs[l])
        identf = sb("identf", [128, 128], F32)
        C.identb = sb("identb", [128, 128], BF16)
        P.dma("sp", identf[:], C.identf)
        P.copy("dve", C.identb[:], identf[:])
        cq = sb("cq", [128, 3, S], BF16)
        ckv = sb("ckv", [128, 2, S], BF16)
        P.dma("sp", cq[:], C.mcq.rearrange("(c p) n -> p c n", p=128))
        P.dma("sp", ckv[:], C.mckv.rearrange("(c p) n -> p c n", p=128))
        cosm = sb("cosm", [128, S], BF16)
        sinm = sb("sinm", [128, S], BF16)
        P.dma("sp", cosm[:], C.tabs[2])
        P.dma("sp", sinm[:], C.tabs[3])
        wuq = sb("wuq", [128, 3, 1536], BF16)
        wukv = sb("wukv", [128, 2, 1024], BF16)
        va = sb("va", [128, NT, 8 * 65 + 63], BF16)
        stg = [sb("stg%d" % i, [128, 1536], F32) for i in range(2)]
        for c in range(3):
            P.dma("sp", stg[c % 2][:], C.wuqx[l, c * 128:(c + 1) * 128, :])
            P.ts(("pool", "dve")[c % 2], wuq[:, c, :], stg[c % 2][:], vp[:, V_QN + c:V_QN + c + 1], None, ALU.mult)
        for c in range(2):
            P.dma("sp", stg[(c + 1) % 2][:, 0:1024], C.wukvx[l, c * 128:(c + 1) * 128, :])
            P.ts(("dve", "pool")[c % 2], wukv[:, c, :], stg[(c + 1) % 2][:, 0:1024], vp[:, V_KVN + c:V_KVN + c + 1], None, ALU.mult)
        B = AttnBufs(C, es, "d%d_" % l)
        pA = [ps("pA%d" % i, [128, 512], F32) for i in range(2)]
        pK = pA[0]
        P.memset("pool", va[:, :, 520:583], 0.0)
        vav = va[:, :, 0:520].rearrange("p n (h d) -> p n h d", h=8)
        P.memset("pool", vav[:, :, :, 64:65], 1.0)
        for i in range(NT):
            pp = pA[i % 2]
            for c in range(2):
                P.mm(pp[:], ckv[:, c, i * 128:(i + 1) * 128], wukv[:, c, 512:1024], start=(c == 0), stop=(c == 1))
            P.copy("act" if i % 2 else "dve", vav[:, i, :, 0:64], pp[:].rearrange("p (h d) -> p h d", h=8))
        qT = [sb("qT%d" % i, [128, S], BF16) for i in range(2)]
        kT = [sb("kT%d" % i, [128, S], BF16) for i in range(2)]
        for i in range(2):
            P.memset("pool", qT[i][96:128, :], 0.0)
            P.memset("pool", kT[i][96:128, :], 0.0)
        t1 = [sb("t1%d" % i, [96, 512], F32) for i in range(2)]
        t2 = [sb("t2%d" % i, [96, 512], F32) for i in range(2)]
        for i in range(2):
            P.dma("sp", kT[i][64:96, :], C.mkr)
        cntn = [0]

        def proj(h, qlist=None, part=3):
            k = h % 2
            n = cntn[0]
            for q in (range(NB) if qlist is None else qlist):
                qs = slice(q * 512, (q + 1) * 512)
                if part == 2:
                    for c in range(2):
                        P.mm(pK[0:64, :], wukv[:, c, h * 64:(h + 1) * 64], ckv[:, c, qs], start=(c == 0), stop=(c == 1))
                    P.copy("dve", kT[k][0:64, qs], pK[0:64, :])
                    continue
                pa, pb = pA[0], pA[1]
                for c in range(3):
                    P.mm(pa[0:96, :], wuq[:, c, h * 192:h * 192 + 96], cq[:, c, qs], start=(c == 0), stop=(c == 2))
                for c in range(3):
                    P.mm(pb[0:96, :], wuq[:, c, h * 192 + 96:h * 192 + 192], cq[:, c, qs], start=(c == 0), stop=(c == 2))
                a1, a2 = t1[n % 2], t2[n % 2]
                n += 1
                cntn[0] = n
                P.tt("dve", a1[64:96, :], pb[64:96, :], sinm[64:96, qs], ALU.mult)
                P.tt("dve", a2[64:96, :], pa[64:96, :], cosm[64:96, qs], ALU.mult)
                P.copy("dve", qT[k][0:64, qs], pa[0:64, :])
                P.tt("pool", qT[k][64:96, qs], a1[64:96, :], a2[64:96, :], ALU.add)
                if part == 1:
                    continue
                pk = pK
                for c in range(2):
                    P.mm(pk[0:64, :], wukv[:, c, h * 64:(h + 1) * 64], ckv[:, c, qs], start=(c == 0), stop=(c == 1))
                P.copy("dve", kT[k][0:64, qs], pk[0:64, :])

        proj(0)
        for h in range(8):
            k = h % 2
            if h + 1 < 8:
                for q in range(NB):
                    B.defer((lambda h=h, q=q: proj(h + 1, [q], 1)), 3 + 14 * q)
                    B.defer((lambda h=h, q=q: proj(h + 1, [q], 2)), 10 + 14 * q)
            dense_causal(C, B, qT[k], kT[k], 128, va[:, :, h * 65:h * 65 + 128], 96 ** -0.5, C.oT[2][h * 64:(h + 1) * 64, :])
        B.drain()
        P.flush()


def phase_b(C, l):
    nc, P = C.nc, C.P
    GC = 0.7978845608028654
    with contextlib.ExitStack() as es:
        sb = lambda n, s, d: es.enter_context(nc.sbuf_tensor("b%d_" % l + n, s, d))
        ps = lambda n, s, d: es.enter_context(nc.psum_tensor("b%d_" % l + n, s, d))
        vp = sb("vp", [128, NVEC], F32)
        P.dma("sp", vp[:], C.vecs[l])
        identf = sb("identf", [128, 128], F32)
        C.identb = sb("identb", [128, 128], BF16)
        P.dma("sp", identf[:], C.identf)
        P.copy("dve", C.identb[:], identf[:])
        kcT = [sb("kcT%d" % g, [64, 256], BF16) for g in range(2)]
        vca = [sb("vca%d" % g, [128, 2, 129], BF16) for g in range(2)]
        selq = [sb("selq%d" % i, [64, 512], BF16) for i in range(2)]
        gsb = sb("gsb", [128, NT, 24], F32)
        P.dma("sp", gsb[:], C.ng.rearrange("(n p) c -> p n c", p=128))
        with contextlib.ExitStack() as es1:
            sb1 = lambda n, s, d: es1.enter_context(nc.sbuf_tensor("b%d1_" % l + n, s, d))
            w1s = sb1("w1s", [64, 32, 256], F32)
            w1 = [sb1("w1%d" % i, [64, 32, 256], BF16) for i in range(2)]
            w2s = sb1("w2s", [128, 2, 64], F32)
            w2 = [sb1("w2%d" % i, [128, 2, 64], BF16) for i in range(2)]
            ovs = sb1("ovs", [128, 2, 64], F32)
            srcs = [sb1("src%d" % i, [64, S], BF16) for i in range(2)]
            peb = [sb1("peb%d" % i, [64, 32], BF16) for i in range(2)]
            cb = sb1("cb", [128, 4], F32)
            hx = sb1("hx", [128, 256], F32)
            h2 = sb1("h2", [128, 256], F32)
            h3 = sb1("h3", [128, 256], F32)
            gl = sb1("gl", [128, 2, 256], BF16)
            ps1 = lambda n, s, d: es1.enter_context(nc.psum_tensor("b%d1_" % l + n, s, d))
            ph = [ps1("ph%d" % i, [128, 512], F32) for i in range(2)]
            pk = ps1("pk", [128, 512], F32)
            pcb = ps1("pcb", [128, 512], F32)
            for kv, (w1d, w2d) in enumerate(((C.cw1k, C.cw2k), (C.cw1v, C.cw2v))):
                P.dma("sp", w1s[:], w1d[l].rearrange("(l d) n -> d l n", d=64))
                for q4 in range(4):
                    P.copy(("pool", "dve", "act", "pool")[q4], w1[kv][:, q4 * 8:(q4 + 1) * 8, :], w1s[:, q4 * 8:(q4 + 1) * 8, :])
                P.dma("sp", w2s[:], w2d[l].rearrange("(c p) n -> p c n", p=128))
                P.copy("dve", w2[kv][:], w2s[:])
                P.copy("dve", peb[kv][:], vp[0:64, (V_PEK if kv == 0 else V_PEV):(V_PEK if kv == 0 else V_PEV) + 32])
            P.dma("sp", ovs[:], C.ovl.rearrange("(c p) n -> p c n", p=128))
            P.memset("pool", gl[:], 0.0)
            for g in range(2):
                P.memset("pool", kcT[g][:], 0.0)
                P.copy("pool", vca[g][:, :, 65:129], ovs[:])
                P.memset("pool", vca[g][:, :, 64:65], 1.0)
            for kv in range(2):
                for hc in range(2):
                    i = kv * 2 + hc
                    for li in range(32):
                        P.mm(pcb[:, i:i + 1], w1[kv][:, li, hc * 128:(hc + 1) * 128], peb[kv][:, li:li + 1], start=(li == 0), stop=(li == 31))
            P.copy("dve", cb[:], pcb[:, 0:4])
            nsrc = 0
            for kv in range(2):
                for g in range(2):
                    src = srcs[nsrc % 2]
                    nsrc += 1
                    P.dma("sp", src[:], (C.nkc if kv == 0 else C.nvc)[g * 64:(g + 1) * 64, :])
                    for hc in range(2):
                        for li in range(32):
                            P.mm(ph[hc][:, 0:255], w1[kv][:, li, hc * 128:(hc + 1) * 128], src[:, li:li + 16 * 254 + 1:16], start=(li == 0), stop=(li == 31))
                        P.ts("dve", hx[:, 0:255], ph[hc][:, 0:255], cb[:, kv * 2 + hc:kv * 2 + hc + 1], None, ALU.add)
                        P.act(h2[:, 0:255], hx[:, 0:255], AF.Square)
                        P.ts("dve", h2[:, 0:255], h2[:, 0:255], 0.044715, 1.0, ALU.mult, ALU.add)
                        P.tt("dve", h2[:, 0:255], h2[:, 0:255], hx[:, 0:255], ALU.mult)
                        P.act(h3[:, 0:255], h2[:, 0:255], AF.Tanh, scale=GC)
                        P.ts("dve", h3[:, 0:255], h3[:, 0:255], 1.0, 0.5, ALU.add, ALU.mult)
                        P.tt("dve", gl[:, hc, 0:255], h3[:, 0:255], hx[:, 0:255], ALU.mult)
                    if kv == 0:
                        for hc in range(2):
                            P.mm(pk[0:64, 0:255], w2[0][:, hc, :], gl[:, hc, 0:255], start=(hc == 0), stop=(hc == 1))
                        P.copy("dve", kcT[g][:, 0:255], pk[0:64, 0:255])
                    else:
                        for c in range(2):
                            for hc in range(2):
                                P.mm(pk[:, c * 64:(c + 1) * 64], gl[:, hc, c * 128:(c + 1) * 128], w2[1][:, hc, :],
                                     start=(hc == 0 and c == 0), stop=(hc == 1 and c == 1))
                        P.copy("dve", vca[g][:, :, 0:64], pk[:, 0:128].rearrange("p (c d) -> p c d", c=2))
            fox_decay(C, l, lambda n, s_, d: sb1("fx" + n, s_, d), vp)
            P.flush()
        with contextlib.ExitStack() as es1:
            sb1 = lambda n, s, d: es1.enter_context(nc.sbuf_tensor("b%d2_" % l + n, s, d))
            cmk = sb1("cmk", [128, 2, S], BF16)
            P.dma("sp", cmk[:], C.cmpm)
            fmul = sb1("fmul", [128, NT, 64], F32)
            fadd = sb1("fadd", [128, NT, 64], F32)
            P.dma("sp", fmul[:], C.forc[0].rearrange("p (n j) -> p n j", j=64))
            P.dma("sp", fadd[:], C.forc[1].rearrange("p (n j) -> p n j", j=64))
            qg = [sb1("qg%d" % i, [64, S], BF16) for i in range(4)]
            ps1 = lambda n, s, d: es1.enter_context(nc.psum_tensor("b%d2_" % l + n, s, d))
            pss = [ps1("pss%d" % i, [128, 512], F32) for i in range(2)]
            pcxs = [[ps1("pcx%d_%d" % (k, i), [128, 512], F32) for i in range(2)] for k in range(2)]
            pT = ps1("pT2", [128, 1024], BF16)
            pTf = ps1("pTf", [128, 512], F32)
            ocT = [sb1("ocT%d" % i, [64, 512], F32) for i in range(2)]
            pt = [sb1("pt%d" % i, [128, 512], BF16) for i in range(2)]
            impacc = sb1("impacc", [128, 4, 64], F32)
            imptmp = sb1("imptmp", [128, 4, 64], F32)
            oc = [sb1("oc%d" % i, [128, 4, 64], F32) for i in range(2)]
            impm = sb1("impm", [128, 4, 64], F32)
            imp2 = sb1("imp2", [128, 4, 64], F32)
            t8 = sb1("t8", [128, 4, 16], F32)
            selb = sb1("selb", [128, 4, 64], BF16)
            pipe = Pipe(1)
            st = {"nsx": 0, "noc": 0}
            smb = [sb1("smb%d" % i, [128, 16], F32) for i in range(2)]
            for g in range(2):
                for r in range(4):
                    P.dma("sp", qg[r][:], C.nq[(4 * g + r) * 64:(4 * g + r + 1) * 64, :])
                for q in range(NB):
                    qs = slice(q * 512, (q + 1) * 512)
                    nch = 1 if q < 4 else 2
                    for r in range(4):
                        h = 4 * g + r
                        pset = pcxs[(st["noc"]) % 2]
                        o_c = oc[st["noc"] % 2]
                        o_t = ocT[st["noc"] % 2]
                        sm = smb[st["noc"] % 2]
                        st["noc"] += 1
                        for c in range(nch):
                            holder = {}

                            def qk_fn(c=c, g=g, r=r, qs=qs, holder=holder):
                                p_s, p_t = pss[st["nsx"] % 2], pt[st["nsx"] % 2]
                                st["nsx"] += 1
                                P.mm(p_s[:], kcT[g][:, c * 128:(c + 1) * 128], qg[r][:, qs], start=True, stop=False)
                                P.mm(p_s[:], C.identb[:], cmk[:, c, qs], start=False, stop=True)
                                P.act(p_t[:], p_s[:], AF.Exp, scale=0.125)
                                holder["pt"] = p_t

                            def pv_fn(c=c, g=g, r=r, h=h, q=q, qs=qs, nch=nch, holder=holder, pset=pset, o_c=o_c, o_t=o_t, sm=sm):
                                p_t = holder["pt"]
                                for j in range(4):
                                    P.mm(pset[j // 2][:, (j % 2) * 129:(j % 2) * 129 + 129], p_t[:, j * 128:(j + 1) * 128], vca[g][:, c, :],
                                         start=(c == 0 and j % 2 == 0), stop=(c == nch - 1 and j % 2 == 1))
                                if c != nch - 1:
                                    return
                                for b in range(2):
                                    pc3 = pset[b][:, 0:258].rearrange("p (j c) -> p j c", j=2)
                                    den2 = pset[b][:, 64:194:129]
                                    j0 = 2 * b
                                    tt0 = 4 * q + j0
                                    P.ts("dve", sm[:, j0:j0 + 2], den2, 1e-30, None, ALU.max)
                                    P.recip(sm[:, 4 + j0:6 + j0], sm[:, j0:j0 + 2])
                                    P.tt("dve", sm[:, 8 + j0:10 + j0], sm[:, 4 + j0:6 + j0], gsb[:, tt0:tt0 + 2, h * 3], ALU.mult)
                                    P.tt("dve", o_c[:, j0:j0 + 2, :], pc3[:, :, 0:64],
                                         sm[:, 8 + j0:10 + j0].unsqueeze(2).broadcast_to([128, 2, 64]), ALU.mult)
                                    rbc = sm[:, 4 + j0:6 + j0].unsqueeze(2).broadcast_to([128, 2, 64])
                                    if r == 0:
                                        P.tt("dve", impacc[:, j0:j0 + 2, :], pc3[:, :, 65:129], rbc, ALU.mult)
                                    else:
                                        P.tt("dve", imptmp[:, j0:j0 + 2, :], pc3[:, :, 65:129], rbc, ALU.mult)
                                        P.tt("pool", impacc[:, j0:j0 + 2, :], impacc[:, j0:j0 + 2, :], imptmp[:, j0:j0 + 2, :], ALU.add)

                                def e2():
                                    for j in range(4):
                                        P.tr(pTf[0:64, j * 128:(j + 1) * 128], o_c[:, j, :], identf[:])
                                    P.copy("act", o_t[:], pTf[0:64, :])
                                    P.dma("sp", C.ocmpT[h * 64:(h + 1) * 64, qs], o_t[:])
                                pipe.defer(e2, 2)
                                if r != 3:
                                    return
                                P.tt("dve", impm[:], impacc[:], fmul[:, 4 * q:4 * q + 4, :], ALU.mult)
                                P.tt("dve", impm[:], impm[:], fadd[:, 4 * q:4 * q + 4, :], ALU.add)
                                for j in range(4):
                                    P.add("dve", lambda e, j=j: e.max(t8[:, j, 0:8], impm[:, j, :]), [impm[:, j, :]], [t8[:, j, 0:8]])
                                    P.add("dve", lambda e, j=j: e.match_replace(imp2[:, j, :], t8[:, j, 0:8], impm[:, j, :], -3.0e9),
                                          [t8[:, j, 0:8], impm[:, j, :]], [imp2[:, j, :]])
                                    P.add("dve", lambda e, j=j: e.max(t8[:, j, 8:16], imp2[:, j, :]), [imp2[:, j, :]], [t8[:, j, 8:16]])
                                P.tt("dve", imp2[:], impm[:], t8[:, :, 15:16].broadcast_to([128, 4, 64]), ALU.is_ge)
                                P.ts("dve", selb[:], imp2[:], -1.0, -NEG, ALU.add, ALU.mult)

                                def e3():
                                    for j in range(4):
                                        P.tr(pT[0:64, j * 128:(j + 1) * 128], selb[:, j, :], C.identb[:])
                                    P.copy("dve", selq[q % 2][:], pT[0:64, 0:512])
                                    P.dma("sp", C.selTd[g, :, qs], selq[q % 2][:])
                                pipe.defer(e3, 2)
                            pipe.pair(qk_fn, pv_fn)
            pipe.drain()
            P.flush()
        with contextlib.ExitStack() as es1:
            sb1 = lambda n, s, d: es1.enter_context(nc.sbuf_tensor("b%d3_" % l + n, s, d))
            B = AttnBufs(C, es1, "b%d3_" % l, npo=4)
            kse = [sb1("kse%d" % g, [128, S], BF16) for g in range(2)]
            kwz = [sb1("kwz%d" % g, [128, S], BF16) for g in range(2)]
            vsa = [sb1("vsa%d" % g, [128, NT, 128], BF16) for g in range(2)]
            vwa = [sb1("vwa%d" % g, [128, NT, 128], BF16) for g in range(2)]
            for g in range(2):
                P.memset("pool", vsa[g][:, :, 64:128], 0.0)
                P.memset("pool", vwa[g][:, :, 64:128], 0.0)
                P.dma("sp", kse[g][0:64, :], C.nks[g * 64:(g + 1) * 64, :])
                P.dma("sp", kse[g][64:128, :], C.esel)
                P.dma("sp", kwz[g][0:64, :], C.nkw[g * 64:(g + 1) * 64, :])
                P.memset("pool", kwz[g][64:128, :], 0.0)
                P.dma("sp", vsa[g][:, :, 0:64], C.nvs[:, g * 64:(g + 1) * 64].rearrange("(n p) d -> p n d", p=128))
                P.dma("sp", vwa[g][:, :, 0:64], C.nvw[:, g * 64:(g + 1) * 64].rearrange("(n p) d -> p n d", p=128))
                P.memset("pool", vsa[g][:, :, 64:65], 1.0)
                P.memset("pool", vwa[g][:, :, 64:65], 1.0)
            qr = [sb1("qr%d" % i, [128, S], BF16) for i in range(2)]
            ocl = [sb1("ocl%d" % i, [64, 512], F32) for i in range(3)]
            gq = [sb1("gq%d" % i, [65, 2, 512], F32) for i in range(3)]
            nq_ = 0
            def loadq(h):
                P.dma("sp", qr[h % 2][0:64, :], C.nqr[h * 64:(h + 1) * 64, :])
                P.dma("sp", qr[h % 2][64:128, :], C.selTd[h // 4])
            loadq(0)
            for h in range(8):
                g = h // 4
                q_r = qr[h % 2]
                if h + 1 < 8:
                    loadq(h + 1)
                for q in range(NB):
                    qs = slice(q * 512, (q + 1) * 512)
                    k2 = nq_ % 3
                    nq_ += 1
                    P.dma("sp", ocl[k2][:], C.ocmpT[h * 64:(h + 1) * 64, qs])
                    P.dma("sp", gq[k2][64:65, 0, :], C.ngT[h * 3 + 1:h * 3 + 2, qs])
                    P.dma("sp", gq[k2][64:65, 1, :], C.ngT[h * 3 + 2:h * 3 + 3, qs])
                    po_s, os_ = B.new_acc()
                    po_w, ow_ = B.new_acc()
                    nk = 4 * q + 4
                    for kt in range(nk):
                        m = kt - 4 * q
                        c0 = 128 * max(m, 0)
                        holder = {}

                        def qk_fn(kt=kt, m=m, c0=c0, qs=qs, holder=holder, g=g, q_r=q_r):
                            extra = []
                            if m >= 0:
                                extra.append((C.identb[:], B.masks[:, m, :], 128 * m, 128 * m + 128))
                            holder["pt"] = B.score(kse[g][:, kt * 128:(kt + 1) * 128], q_r[:, qs], c0, 512, 0.125, extra)

                        def pv_fn(kt=kt, c0=c0, nk=nk, holder=holder, g=g, po_s=po_s, os_=os_, k2=k2, q=q):
                            P.mm(po_s[:, c0:512], vsa[g][:, kt, :], holder["pt"][:, c0:512], start=(kt == 0), stop=(kt == nk - 1))
                            if kt == nk - 1:
                                attn_finish(C, B, po_s, os_, None, q, gate=gq[k2][64:65, 0, :])
                        B.pair(qk_fn, pv_fn)
                    kts = [4 * q] + list(range(max(0, 4 * q - 4), 4 * q)) + list(range(4 * q + 1, nk))
                    k0, klast = kts[0], kts[-1]
                    for kt in kts:
                        mp = kt - (4 * q - 4)
                        if mp < 4:
                            c0, c1, mi, e0 = 0, 128 * (mp + 1), 4 + mp, 128 * mp
                        else:
                            c0, c1, mi, e0 = 128 * (mp - 4), 512, mp - 4, 128 * (mp - 4)
                        holder = {}

                        def qk_fn(kt=kt, c0=c0, c1=c1, mi=mi, e0=e0, qs=qs, holder=holder, g=g, q_r=q_r):
                            holder["pt"] = B.score(kwz[g][:, kt * 128:(kt + 1) * 128], q_r[:, qs], c0, c1, 0.125,
                                                   [(C.identb[:], B.masks[:, mi, :], e0, e0 + 128)])

                        def pv_fn(kt=kt, c0=c0, c1=c1, k0=k0, klast=klast, holder=holder, g=g, po_w=po_w, ow_=ow_, os_=os_, k2=k2, q=q, h=h):
                            P.mm(po_w[:, c0:c1], vwa[g][:, kt, :], holder["pt"][:, c0:c1], start=(kt == k0), stop=(kt == klast))
                            if kt == klast:
                                attn_finish(C, B, po_w, ow_, C.oT[0][h * 64:(h + 1) * 64, :], q, gate=gq[k2][64:65, 1, :],
                                            addT=[os_[0:64, :], ocl[k2][:]])
                        B.pair(qk_fn, pv_fn)
            B.drain()
            P.flush()


def load_cast(C, P, dst_bf, src_dram, stage, rows_chunks, ncols, eng="pool"):
    P.dma("sp", stage[:, 0:rows_chunks, 0:ncols], src_dram.rearrange("(c p) n -> p c n", p=128))
    for c in range(rows_chunks):
        P.copy(("pool", "act", "dve")[c % 3], dst_bf[:, c, 0:ncols], stage[:, c, 0:ncols])


def phase_e(C, l, xin, xout):
    nc, P = C.nc, C.P
    with contextlib.ExitStack() as es:
        sb = lambda n, s, d: es.enter_context(nc.sbuf_tensor("e%d_" % l + n, s, d))
        ps = lambda n, s, d: es.enter_context(nc.psum_tensor("e%d_" % l + n, s, d))
        stage = sb("stage", [128, 4, 1024], F32)
        wbr = [sb("wbr%d" % i, [128, 4, 1024], BF16) for i in range(3)]
        wout = sb("wout", [128, 8, 1024], BF16)
        for i in range(3):
            load_cast(C, P, wbr[i], C.wbr[l, i], stage, 4, 1024)
        for hf in range(2):
            P.dma("sp", stage[:, :, :], C.wout[l, hf * 512:(hf + 1) * 512, :].rearrange("(c p) n -> p c n", p=128))
            for c in range(4):
                P.copy(("pool", "act", "dve")[c % 3], wout[:, hf * 4 + c, :], stage[:, c, :])
        oT = [[sb("oT%d_%d" % (i, k), [128, 4, 512], BF16) for i in range(3)] for k in range(2)]
        gt = [[sb("g%d_%d" % (i, k), [128, 512], BF16) for i in range(3)] for k in range(4)]
        pbr = [[ps("pbr%d_%d" % (i, k), [128, 512], F32) for i in range(3)] for k in range(2)]
        po = [ps("po%d" % k, [128, 512], F32) for k in range(2)]
        m1 = [sb("m1_%d" % k, [128, 512], F32) for k in range(2)]
        m2 = [sb("m2_%d" % k, [128, 512], F32) for k in range(2)]
        m3 = [sb("m3_%d" % k, [128, 512], F32) for k in range(2)]
        mT = [sb("mT%d" % k, [128, 8, 512], BF16) for k in range(2)]
        xt = [sb("xt%d" % k, [128, D], F32) for k in range(2)]
        n = 0
        nx = 0

        def load_oT(q):
            for i in range(3):
                P.dma("sp", oT[q % 2][i][:], C.oT[i][:, q * 512:(q + 1) * 512].rearrange("(c p) n -> p c n", p=128))

        def load_g(q, fc):
            k4 = (q * 8 + fc) % 4
            for i in range(3):
                P.dma("sp", gt[k4][i][:], C.gm[i * 1024 + fc * 128:i * 1024 + (fc + 1) * 128, q * 512:(q + 1) * 512])

        st_e = {"nx": 0}

        def out_proj(q):
            for j in range(4):
                nx = st_e["nx"]
                x_t = xt[nx % 2]
                tsl = slice(q * 512 + j * 128, q * 512 + (j + 1) * 128)
                P.dma("sp", x_t[:], xin[tsl, :])
                for hf in range(2):
                    pp = po[hf]
                    for fc in range(8):
                        P.mm(pp[:], mT[q % 2][:, fc, j * 128:(j + 1) * 128], wout[:, fc, hf * 512:(hf + 1) * 512], start=(fc == 0), stop=(fc == 7))
                    P.tt("dve", x_t[:, hf * 512:(hf + 1) * 512], pp[:], x_t[:, hf * 512:(hf + 1) * 512], ALU.add)
                P.dma("act", xout[tsl, :], x_t[:])
                st_e["nx"] = nx + 1

        jobs = [(q, fc) for q in range(NB) for fc in range(8)]
        load_oT(0)
        load_g(*jobs[0])
        load_g(*jobs[1])
        for q in range(NB):
            qs = slice(q * 512, (q + 1) * 512)
            if q + 1 < NB:
                load_oT(q + 1)
            for fc in range(8):
                k = n % 2
                k4 = n % 4
                if n + 2 < len(jobs):
                    load_g(*jobs[n + 2])
                n += 1
                for i in range(3):
                    for c in range(4):
                        P.mm(pbr[k][i][:], wbr[i][:, c, fc * 128:(fc + 1) * 128], oT[q % 2][i][:, c, :], start=(c == 0), stop=(c == 3))
                P.tt("dve", m1[k][:], pbr[k][0][:], gt[k4][0][:], ALU.mult)
                P.tt("dve", m2[k][:], pbr[k][1][:], gt[k4][1][:], ALU.mult)
                P.tt("dve", m3[k][:], pbr[k][2][:], gt[k4][2][:], ALU.mult)
                P.tt("pool", m1[k][:], m1[k][:], m2[k][:], ALU.add)
                P.tt("pool", mT[q % 2][:, fc, :], m1[k][:], m3[k][:], ALU.add)
                if fc == 1 and q >= 1:
                    out_proj(q - 1)
        out_proj(NB - 1)
        P.flush()


def phase_f(C, l, xin, xout, final):
    nc, P = C.nc, C.P
    with contextlib.ExitStack() as es:
        sb = lambda n, s, d: es.enter_context(nc.sbuf_tensor("f%d_" % l + n, s, d))
        ps = lambda n, s, d: es.enter_context(nc.psum_tensor("f%d_" % l + n, s, d))
        C.vp = sb("vp", [128, NVEC], F32)
        C.epsb = sb("epsb", [128, 1], F32)
        identf = sb("identf", [128, 128], F32)
        C.identb = sb("identb", [128, 128], BF16)
        hT = sb("hT", [128, 8, S], BF16)
        P.dma("sp", C.vp[:], C.vecs[l])
        P.dma("sp", identf[:], C.identf)
        P.memset("dve", C.epsb[:], EPS)
        P.copy("dve", C.identb[:], identf[:])
        norm_T(C, l, xin, hT, V_FFNN, "f%dn_" % l)
        with contextlib.ExitStack() as es1:
            sb1 = lambda n, s, d: es1.enter_context(nc.sbuf_tensor("f%d_" % l + n, s, d))
            wst = [sb1("wst%d" % i, [128, 8, 512], F32) for i in range(2)]
            wbf = [sb1("wbf%d" % i, [128, 8, 512], BF16) for i in range(2)]
            pu = [ps("pu%d" % i, [128, 512], F32) for i in range(3)]
            pv = [ps("pv%d" % i, [128, 512], F32) for i in range(3)]
            U = [sb1("U%d" % i, [128, S + 2], F32) for i in range(2)]
            for i in range(2):
                P.memset("dve", U[i][:, 0:2], 0.0)
            tt_ = [sb1("t%d" % i, [128, 512], F32) for i in range(3)]
            sg = [sb1("sg%d" % i, [128, 512], F32) for i in range(3)]
            yb = [sb1("yb%d" % i, [128, 512], BF16) for i in range(3)]
            n = 0
            def loadw(g):
                w_s, w_b = wst[g % 2], wbf[g % 2]
                P.dma("sp", w_s[:], C.wupx[l, :, g * 512:(g + 1) * 512].rearrange("(c p) n -> p c n", p=128))
                for c in range(8):
                    P.copy("act", w_b[:, c, :], w_s[:, c, :])
            loadw(0)
            for g in range(NFC // 2):
                w_b = wbf[g % 2]
                if g + 1 < NFC // 2:
                    loadw(g + 1)
                for f2 in range(2):
                    fc = 2 * g + f2
                    for q in range(NB):
                        k = n % 3
                        n += 1
                        qs = slice(q * 512, (q + 1) * 512)
                        for c in range(8):
                            P.mm(pu[k][:], w_b[:, c, f2 * 256:f2 * 256 + 128], hT[:, c, qs], start=(c == 0), stop=(c == 7))
                        for c in range(8):
                            P.mm(pv[k][:], w_b[:, c, f2 * 256 + 128:f2 * 256 + 256], hT[:, c, qs], start=(c == 0), stop=(c == 7))
                        u = U[fc % 2]
                        b0 = q * 512
                        P.copy("act", u[:, b0 + 2:b0 + 514], pu[k][:])
                        t = tt_[k]
                        P.ts("dve", t[:], u[:, b0 + 2:b0 + 514], C.vp[:, V_CW2 + fc:V_CW2 + fc + 1], C.vp[:, V_CB + fc:V_CB + fc + 1], ALU.mult, ALU.add)
                        P.stt("dve", t[:], u[:, b0 + 1:b0 + 513], C.vp[:, V_CW1 + fc:V_CW1 + fc + 1], t[:], ALU.mult, ALU.add)
                        P.stt("dve", t[:], u[:, b0:b0 + 512], C.vp[:, V_CW0 + fc:V_CW0 + fc + 1], t[:], ALU.mult, ALU.add)
                        P.act(sg[k][:], t[:], AF.Silu)
                        P.tt("dve", yb[k][:], pv[k][:], sg[k][:], ALU.mult)
                        P.dma("sp", C.yT[fc * 128:(fc + 1) * 128, qs], yb[k][:])
            P.flush()
    with contextlib.ExitStack() as es:
        sb = lambda n, s, d: es.enter_context(nc.sbuf_tensor("g%d_" % l + n, s, d))
        ps = lambda n, s, d: es.enter_context(nc.psum_tensor("g%d_" % l + n, s, d))
        stages = [sb("stage%d" % i, [128, 11, 512], F32) for i in range(2)]
        wd = sb("wd", [128, NFC, 1024], BF16)
        for hf in range(2):
            for cc in range(2):
                stage = stages[cc]
                P.dma("sp", stage[:], C.wdown[l, cc * 1408:(cc + 1) * 1408, hf * 512:(hf + 1) * 512].rearrange("(c p) n -> p c n", p=128))
                for c in range(11):
                    P.copy(("pool", "act", "dve")[c % 3], wd[:, cc * 11 + c, hf * 512:(hf + 1) * 512], stage[:, c, :])
        yT = [sb("yT%d" % k, [128, NFC, 512], BF16) for k in range(2)]
        xt = [sb("xt%d" % k, [128, D], F32) for k in range(2)]
        po = [ps("po%d" % k, [128, 512], F32) for k in range(2)]
        if final:
            fin = sb("fin", [128, D], F32)
            P.dma("sp", fin[:], C.fin)
            epsb = sb("epsb", [128, 1], F32)
            P.memset("dve", epsb[:], EPS)
            junk = sb("junk", [128, D], BF16)
            st = sb("st", [128, 4], F32)
        nx = 0
        P.dma("sp", yT[0][:], C.yT[:, 0:512].rearrange("(c p) n -> p c n", p=128))
        for q in range(NB):
            qs = slice(q * 512, (q + 1) * 512)
            if q + 1 < NB:
                P.dma("sp", yT[(q + 1) % 2][:], C.yT[:, (q + 1) * 512:(q + 2) * 512].rearrange("(c p) n -> p c n", p=128))
            for j in range(4):
                x_t = xt[nx % 2]
                tsl = slice(q * 512 + j * 128, q * 512 + (j + 1) * 128)
                P.dma("sp", x_t[:], xin[tsl, :])
                for hf in range(2):
                    pp = po[hf]
                    for fc in range(NFC):
                        P.mm(pp[:], yT[q % 2][:, fc, j * 128:(j + 1) * 128], wd[:, fc, hf * 512:(hf + 1) * 512], start=(fc == 0), stop=(fc == NFC - 1))
                    P.tt("dve", x_t[:, hf * 512:(hf + 1) * 512], pp[:], x_t[:, hf * 512:(hf + 1) * 512], ALU.add)
                if final:
                    c0 = nx % 2
                    P.act(junk[:], x_t[:], AF.Square, accum_out=st[:, c0:c0 + 1])
                    P.act(st[:, 2 + c0:3 + c0], st[:, c0:c0 + 1], AF.Sqrt, bias=epsb[:], scale=1.0 / D)
                    P.recip(st[:, c0:c0 + 1], st[:, 2 + c0:3 + c0])
                    P.stt("dve", x_t[:], x_t[:], st[:, c0:c0 + 1], fin[:], ALU.mult, ALU.mult)
                P.dma("act", xout[tsl, :], x_t[:])
                nx += 1
        P.flush()


def _rot_consts():
    rc = np.zeros((128, 4), np.float32)
    half_n = 8
    invn = (np.float32(THETA) ** (-(np.arange(half_n, dtype=np.float32)) / np.float32(half_n))).astype(np.float32)
    half_m = 16
    invm = (np.float32(THETA) ** (-(np.arange(half_m, dtype=np.float32)) / np.float32(half_m))).astype(np.float32)
    for r in range(128):
        d = r % 64
        if d < 16:
            rc[r, 0] = invn[d % 8]
            rc[r, 1] = -1.0 if d < 8 else 1.0
        if d < 32:
            rc[r, 2] = invm[d % 16]
            rc[r, 3] = -1.0 if d < 16 else 1.0
    return rc


def _vec_pack(inp, l):
    v = np.zeros((128, NVEC), np.float32)
    v[:, V_MIXN:V_MIXN + 8] = inp["mix_norm"][l].reshape(8, 128).T
    v[:, V_FFNN:V_FFNN + 8] = inp["ffn_norm"][l].reshape(8, 128).T
    v[:, V_QN:V_QN + 3] = inp["mla_q_norm"][l].reshape(3, 128).T
    v[:, V_KVN:V_KVN + 2] = inp["mla_kv_norm"][l].reshape(2, 128).T
    for k, c0 in enumerate((V_CW0, V_CW1, V_CW2)):
        v[:, c0:c0 + NFC] = inp["conv_w"][l, k].reshape(NFC, 128).T
    v[:, V_CB:V_CB + NFC] = inp["conv_b"][l].reshape(NFC, 128).T
    v[0:8, V_BF] = inp["b_forget"][l]
    v[0:64, V_PEK:V_PEK + 32] = inp["cmp_pe_k"][l].T
    v[0:64, V_PEV:V_PEV + 32] = inp["cmp_pe_v"][l].T
    return v


def prep_shared(inp):
    inp = {k: np.asarray(v) for k, v in inp.items()}
    sh = {}
    cols = np.concatenate([np.asarray(g) for g in WIN_GROUPS])
    sh["wext"] = np.ascontiguousarray(inp["w_in"][:, :, cols])
    sh["vecs"] = np.stack([_vec_pack(inp, l) for l in range(L)])
    sh["rotc"] = _rot_consts()
    sh["identf"] = np.eye(128, dtype=np.float32)
    sh["wbr"] = np.ascontiguousarray(np.stack([inp["w_br_nsa"], inp["w_br_fox"], inp["w_br_mla"]], axis=1))
    sh["wout"] = inp["w_out"]
    ucols = []
    for fc in range(NFC):
        ucols += list(range(fc * 128, (fc + 1) * 128)) + list(range(DFF + fc * 128, DFF + (fc + 1) * 128))
    sh["wupx"] = np.ascontiguousarray(inp["w_up"][:, :, np.asarray(ucols)])
    sh["wdown"] = inp["w_down"]
    mk = np.zeros((8, 128, 512), np.float32)
    sl = np.arange(128)[:, None]
    tl = np.arange(512)[None, :]
    for m in range(4):
        vis = (128 * m + sl) <= tl
        mk[m] = np.where(vis, 0.0, NEG)
        mk[4 + m] = np.where(vis, NEG, 0.0)
    sh["maskc"] = np.ascontiguousarray(mk.transpose(1, 0, 2)).astype(ml_dtypes.bfloat16)
    qc = []
    for h in range(8):
        b0 = h * 96
        qc += list(range(b0, b0 + 96)) + list(range(b0, b0 + 64)) + list(range(b0 + 80, b0 + 96)) + list(range(b0 + 64, b0 + 80))
    sh["wuqx"] = np.ascontiguousarray(inp["mla_w_uq"][:, :, np.asarray(qc)])
    kc = []
    for h in range(8):
        kc += list(range(h * 128, h * 128 + 64))
    for h in range(8):
        kc += list(range(h * 128 + 64, h * 128 + 128))
    sh["wukvx"] = np.ascontiguousarray(inp["mla_w_ukv"][:, :, np.asarray(kc)])
    sh["cw1k"], sh["cw1v"], sh["cw2k"], sh["cw2v"] = inp["cmp_w1_k"], inp["cmp_w1_v"], inp["cmp_w2_k"], inp["cmp_w2_v"]
    n = np.arange(256)
    ov = np.zeros((256, 64), np.float32)
    c0 = 16 * n[:, None]
    s0 = 64 * np.arange(64)[None, :]
    ov[:, :] = np.clip(np.minimum(c0 + 32, s0 + 64) - np.maximum(c0, s0), 0, None) / 32.0
    ov[255] = 0.0
    sh["ovl"] = ov
    t = np.arange(S)
    cm = np.where((16 * n[:, None] + 31) <= t[None, :], 0.0, NEG).astype(np.float32)
    sh["cmpm"] = np.ascontiguousarray(cm.reshape(2, 128, S).transpose(1, 0, 2)).astype(ml_dtypes.bfloat16)
    fc = np.zeros((2, 128, NT, 64), np.float32)
    j = np.arange(64)[None, None, :]
    tt = (np.arange(NT)[None, :, None] * 128 + np.arange(128)[:, None, None])
    cur = tt // 64
    forced = (j == 0) | (j == cur) | (j == cur - 1)
    future = j > cur
    fc[0] = np.where(forced | future, 0.0, 1.0)
    fc[1] = np.where(forced, 1e9, np.where(future, -1e9, 0.0))
    sh["forc"] = fc.reshape(2, 128, NT * 64)
    es_ = np.zeros((64, 32, 128), np.float32)
    for kt in range(32):
        es_[2 * kt, kt, 0:64] = 1.0
        es_[2 * kt + 1, kt, 64:128] = 1.0
    sh["esel"] = np.ascontiguousarray(es_.reshape(64, S)).astype(ml_dtypes.bfloat16)
    sh["fin"] = np.ascontiguousarray(np.broadcast_to(inp["final_norm"][None, :], (128, D))).astype(np.float32)
    return sh


def make_in_maps(inp, ncores=8):
    sh = prep_shared(inp)
    x = np.asarray(inp["x"])
    pos = np.asarray(inp["positions"]).astype(np.int32)
    maps = []
    for b in range(ncores):
        m = dict(sh)
        m["x"] = np.ascontiguousarray(x[b])
        m["pos"] = np.ascontiguousarray(pos[b:b + 1])
        maps.append(m)
    return maps


_NC_CACHE = {}
KDBG = {}


def kernel(**inputs):
    if "nc" not in _NC_CACHE:
        _NC_CACHE["nc"] = build_program(dict(KDBG))
    nc = _NC_CACHE["nc"]
    maps = make_in_maps(inputs, 8)
    res = run_bass_kernel_spmd(nc, maps, core_ids=list(range(8)))
    return np.stack([np.asarray(r["y"]) for r in res.results]).astype(np.float32)
```

```python
import contextlib
import math
import numpy as np
import ml_dtypes
import concourse.bass as bass
import concourse.mybir as mybir
from concourse.bass_utils import run_bass_kernel_spmd

F32 = mybir.dt.float32
BF16 = mybir.dt.bfloat16
I32 = mybir.dt.int32
AF = mybir.ActivationFunctionType
ALU = mybir.AluOpType
AX = mybir.AxisListType

D = 1024
S = 4096
L = 2
NT = S // 128
NB = S // 512
DFF = 2816
NFC = DFF // 128
EPS = 1e-6
THETA = 500000.0
NEG = -30000.0

O_NQ = 0
O_NKV = 512
O_NG = 1280
O_FQ = 1304
O_FK = 1816
O_FV = 2328
O_FF = 2840
O_CQ = 2848
O_CKV = 3232
O_KR = 3488
O_MG = 3520
N_IN = 6592


def _swap_cols_nsa(base, nheads):
    idx = []
    for h in range(nheads):
        b = base + 64 * h
        idx += list(range(b + 8, b + 16)) + list(range(b, b + 8)) + list(range(b + 16, b + 64))
    return idx


def _win_ext_cols():
    groups = []
    for p in range(4):
        a = list(range(O_NQ + 128 * p, O_NQ + 128 * p + 128))
        b = _swap_cols_nsa(O_NQ + 128 * p, 2)
        groups.append(a + b)
    kc = list(range(O_NKV, O_NKV + 128))
    vc = list(range(O_NKV + 128, O_NKV + 256))
    ks = list(range(O_NKV + 256, O_NKV + 384))
    ksb = _swap_cols_nsa(O_NKV + 256, 2)
    groups.append(kc + vc + ks + ksb)
    kw = list(range(O_NKV + 512, O_NKV + 640))
    kwb = _swap_cols_nsa(O_NKV + 512, 2)
    kr = list(range(O_KR, O_KR + 32))
    krb = list(range(O_KR + 16, O_KR + 32)) + list(range(O_KR, O_KR + 16))
    ff = list(range(O_FF, O_FF + 8))
    groups.append(kw + kwb + kr + krb + ff)
    groups.append(list(range(O_FQ, O_FQ + 512)))
    groups.append(list(range(O_FK, O_FK + 512)))
    for i in range(6):
        groups.append(list(range(O_MG + 512 * i, O_MG + 512 * i + 512)))
    vs = list(range(O_NKV + 384, O_NKV + 512))
    vw = list(range(O_NKV + 640, O_NKV + 768))
    ng = list(range(O_NG, O_NG + 24))
    groups.append(vs + vw + ng)
    groups.append(list(range(O_FV, O_FV + 512)))
    groups.append(list(range(O_CQ, O_CQ + 384)))
    groups.append(list(range(O_CKV, O_CKV + 256)))
    return groups


WIN_GROUPS = _win_ext_cols()
WIN_GOFF = np.cumsum([0] + [len(g) for g in WIN_GROUPS]).tolist()
NEXT = WIN_GOFF[-1]

V_MIXN, V_FFNN, V_QN, V_KVN, V_CW0, V_CW1, V_CW2, V_CB, V_BF, V_PEK, V_PEV = 0, 8, 16, 19, 21, 43, 65, 87, 109, 110, 142
NVEC = 174


def _box(ap):
    t = ap.tensor
    shp = t.shape
    rowsize = 1
    for s in shp[1:]:
        rowsize *= s
    off = int(ap.offset)
    r0 = off // rowsize
    c0 = off % rowsize
    rext = 0
    cext = 0
    for step, cnt in ap.ap:
        if cnt <= 1 or step == 0:
            continue
        if step >= rowsize:
            rext += (step // rowsize) * (cnt - 1)
        else:
            cext += step * (cnt - 1)
    c1 = c0 + cext + 1
    if type(t).__name__ == "PSumTensorHandle":
        return (t.name, 0, 128, -2, -1)
    if c1 > rowsize:
        rext += (c1 - 1) // rowsize
        c0, c1 = 0, rowsize
    return (t.name, r0, r0 + rext + 1, c0, c1)


def _ovl(a, b):
    return a[1] < b[2] and b[1] < a[2] and a[3] < b[4] and b[3] < a[4]


def _contains(a, b):
    return a[1] <= b[1] and b[2] <= a[2] and a[3] <= b[3] and b[4] <= a[4]


class _Op:
    __slots__ = ("eng", "fn", "rd", "wr", "dma", "deps", "sig", "sem", "val", "barrier")


class Prog:
    K_DMA = 16

    def __init__(self, nc, es):
        self.nc = nc
        self.engs = {"pe": nc.tensor, "dve": nc.vector, "act": nc.scalar, "pool": nc.gpsimd, "sp": nc.sync}
        self.sems = {k: es.enter_context(nc.semaphore("s_" + k)) for k in ("pe", "dve", "act", "pool")}
        self.dsems = {q: [es.enter_context(nc.semaphore("d_%s%d" % (q, i))) for i in range(self.K_DMA)]
                      for q in ("sp", "act", "pool")}
        self.cnt = {k: 0 for k in self.sems}
        self.dcnt = {q: 0 for q in self.dsems}
        self.seen = {k: {} for k in self.engs}
        self.last = {}
        self.ops = []
        self.n_inst = 0

    def add(self, eng, fn, reads, writes, dma=False):
        o = _Op()
        o.eng, o.fn, o.dma = eng, fn, dma
        o.rd = [_box(a) for a in reads if a is not None and not isinstance(a, (int, float))]
        o.wr = [_box(a) for a in writes if a is not None]
        o.deps, o.sig, o.sem, o.val, o.barrier = None, False, None, 0, False
        self.ops.append(o)
        return o

    def barrier(self):
        for e in self.engs:
            o = _Op()
            o.eng, o.fn, o.dma, o.rd, o.wr = e, None, False, [], []
            o.deps, o.sig, o.sem, o.val, o.barrier = None, False, None, 0, True
            self.ops.append(o)

    def mm(self, out, lhsT, rhs, start=True, stop=True):
        self.add("pe", lambda e: e.matmul(out, lhsT, rhs, start=start, stop=stop), [lhsT, rhs], [out])

    def tr(self, out, in_, ident):
        self.add("pe", lambda e: e.transpose(out, in_, ident), [in_, ident], [out])

    def act(self, out, in_, func, bias=None, scale=None, accum_out=None):
        kw = {}
        if bias is not None:
            kw["bias"] = bias
        if scale is not None:
            kw["scale"] = scale
        if accum_out is not None:
            kw["accum_out"] = accum_out
        self.add("act", lambda e: e.activation(out, in_, func, **kw), [in_, bias, scale], [out, accum_out])

    def ts(self, eng, out, in0, s1, s2=None, op0=ALU.mult, op1=None):
        if op1 is None:
            fn = lambda e: e.tensor_scalar(out, in0, s1, s2, op0)
        else:
            fn = lambda e: e.tensor_scalar(out, in0, s1, s2, op0, op1)
        self.add(eng, fn, [in0, s1, s2], [out])

    def tt(self, eng, out, in0, in1, op):
        self.add(eng, lambda e: e.tensor_tensor(out, in0, in1, op), [in0, in1], [out])

    def stt(self, eng, out, in0, scalar, in1, op0, op1):
        self.add(eng, lambda e: e.scalar_tensor_tensor(out, in0, scalar, in1, op0, op1), [in0, scalar, in1], [out])

    def copy(self, eng, out, in_):
        if eng == "act":
            self.add("act", lambda e: e.copy(out, in_), [in_], [out])
        else:
            self.add(eng, lambda e: e.tensor_copy(out, in_), [in_], [out])

    def amul(self, out, in_, m):
        self.add("act", lambda e: e.mul(out, in_, m), [in_, m], [out])

    def memset(self, eng, out, val):
        self.add(eng, lambda e: e.memset(out, val), [], [out])

    def recip(self, out, in_):
        self.add("dve", lambda e: e.reciprocal(out, in_), [in_], [out])

    def dma(self, q, out, in_, **kw):
        self.add(q, lambda e: e.dma_start(out=out, in_=in_, **kw), [in_], [out], dma=True)

    @staticmethod
    def _need(op, d, raw):
        if d.dma or op.dma:
            return True
        if d.eng != op.eng:
            return True
        if op.eng == "pe":
            return False
        return True

    def flush(self):
        body = self.ops
        self.ops = []
        self.barrier()
        ops = self.ops + body
        self.ops = []
        recs = {}
        for i, o in enumerate(ops):
            if o.barrier:
                o.deps = {}
                recs = {}
                continue
            deps = {}
            for b in o.rd:
                tb = recs.get(b[0])
                if tb:
                    for bb, rec in tb.items():
                        if _ovl(b, bb):
                            if rec[0] is not None:
                                deps[rec[0]] = True
                            if b[4] == -1:
                                for r in rec[1].values():
                                    if r.eng != o.eng:
                                        deps.setdefault(r, False)
            for b in o.wr:
                tb = recs.get(b[0])
                if tb:
                    for bb, rec in tb.items():
                        if _ovl(b, bb):
                            if rec[0] is not None:
                                deps.setdefault(rec[0], False)
                            for r in rec[1].values():
                                deps.setdefault(r, False)
            deps.pop(o, None)
            o.deps = deps
            for b in o.rd:
                tb = recs.setdefault(b[0], {})
                rec = tb.get(b)
                if rec is None:
                    rec = tb[b] = [None, {}]
                rec[1][(o.eng, i) if o.dma else o.eng] = o
            for b in o.wr:
                tb = recs.setdefault(b[0], {})
                for bb in [bb for bb in tb if _contains(b, bb)]:
                    del tb[bb]
                tb[b] = [o, {}]
        for o in ops:
            for d, raw in o.deps.items():
                if self._need(o, d, raw):
                    d.sig = True
        lastop = {}
        for o in ops:
            if not o.dma and not o.barrier and o.eng in self.sems:
                lastop[o.eng] = o
        for o in lastop.values():
            o.sig = True
        for o in ops:
            e = self.engs[o.eng]
            if o.barrier:
                for nm, (sem, val) in list(self.last.items()):
                    if self.seen[o.eng].get(nm, 0) < val:
                        e.wait_ge(sem, val)
                        self.seen[o.eng][nm] = val
                        self.n_inst += 1
                continue
            need = {}
            for d, raw in o.deps.items():
                if self._need(o, d, raw):
                    nm = d.sem.name
                    if need.get(nm, (None, 0))[1] < d.val:
                        need[nm] = (d.sem, d.val)
            for nm, (sem, val) in need.items():
                if self.seen[o.eng].get(nm, 0) < val:
                    e.wait_ge(sem, val)
                    self.seen[o.eng][nm] = val
                    self.n_inst += 1
            if o.dma:
                i = self.dcnt[o.eng]
                self.dcnt[o.eng] = i + 1
                o.sem = self.dsems[o.eng][i % self.K_DMA]
                o.val = 16 * (i // self.K_DMA + 1)
                if o.val > 16 and self.seen[o.eng].get(o.sem.name, 0) < o.val - 16:
                    e.wait_ge(o.sem, o.val - 16)
                    self.seen[o.eng][o.sem.name] = o.val - 16
                    self.n_inst += 1
            elif o.sig:
                self.cnt[o.eng] += 1
                o.sem = self.sems[o.eng]
                o.val = self.cnt[o.eng]
            ins = o.fn(e)
            self.n_inst += 1
            if o.dma:
                ins.then_inc(o.sem, 16)
                self.last[o.sem.name] = (o.sem, o.val)
            elif o.sig:
                ins.then_inc(o.sem, 1)
                self.last[o.sem.name] = (o.sem, o.val)

    def finish(self):
        self.flush()


class Ctx:
    pass


def _dram(nc, name, shape, dt, kind="Internal"):
    return nc.dram_tensor(name, list(shape), dt, kind=kind).ap()


def build_program(dbg=None):
    dbg = dbg or {}
    outs = set(dbg.get("outs", ()))
    nc = bass.Bass("TRN2", target_bir_lowering=False)
    C = Ctx()
    C.nc = nc
    C.dbg = dbg

    def ext_in(name, shape, dt=F32):
        return _dram(nc, name, shape, dt, "ExternalInput")

    def scr(name, shape, dt=BF16):
        return _dram(nc, name, shape, dt, "ExternalOutput" if name in outs else "Internal")

    C.x = ext_in("x", [S, D])
    C.pos = ext_in("pos", [1, S], I32)
    C.wext = ext_in("wext", [L, D, NEXT])
    C.vecs = ext_in("vecs", [L, 128, NVEC])
    C.rotc = ext_in("rotc", [128, 4])
    C.identf = ext_in("identf", [128, 128])
    C.wbr = ext_in("wbr", [L, 3, 512, D])
    C.wout = ext_in("wout", [L, D, D])
    C.wupx = ext_in("wupx", [L, D, 2 * DFF])
    C.wdown = ext_in("wdown", [L, DFF, D])
    C.fin = ext_in("fin", [128, D])
    C.maskc = ext_in("maskc", [128, 8, 512], BF16)
    C.wuqx = ext_in("wuqx", [L, 384, 1536])
    C.cw1k = ext_in("cw1k", [L, 2048, 256])
    C.cw1v = ext_in("cw1v", [L, 2048, 256])
    C.cw2k = ext_in("cw2k", [L, 256, 64])
    C.cw2v = ext_in("cw2v", [L, 256, 64])
    C.ovl = ext_in("ovl", [256, 64])
    C.cmpm = ext_in("cmpm", [128, 2, S], BF16)
    C.forc = ext_in("forc", [2, 128, NT * 64])
    C.esel = ext_in("esel", [64, S], BF16)
    C.ocmpT = scr("ocmpT", [512, S], F32)
    C.ngT = scr("ngT", [24, S], F32)
    C.selTd = scr("selTd", [2, 64, S])
    C.wukvx = ext_in("wukvx", [L, 256, 1024])
    C.y = _dram(nc, "y", [S, D], F32, "ExternalOutput")
    C.oT = [scr("oT%d" % i, [512, S]) for i in range(3)]
    C.yT = scr("yT", [DFF, S])
    C.xmid = scr("xmid", [S, D], F32)
    C.tabs = scr("tabs", [4, 128, S])
    C.xres = scr("xres", [S, D], F32)
    C.nq = scr("nq", [512, S])
    C.nqr = scr("nqr", [512, S])
    C.nkc = scr("nkc", [128, S])
    C.nvc = scr("nvc", [128, S])
    C.nks = scr("nks", [128, S])
    C.nkw = scr("nkw", [128, S])
    C.nvs = scr("nvs", [S, 128])
    C.nvw = scr("nvw", [S, 128])
    C.ng = scr("ng", [S, 24], F32)
    C.fq = scr("fq", [8, 68, S])
    C.fk = scr("fk", [8, 68, S])
    C.fv = scr("fv", [S, 512])
    C.flog = scr("flog", [8, S], F32)
    C.mcq = scr("mcq", [384, S])
    C.mckv = scr("mckv", [256, S])
    C.mkr = scr("mkr", [32, S])
    C.gm = scr("gm", [3072, S])

    with contextlib.ExitStack() as es:
        P = Prog(nc, es)
        C.P = P
        phase_setup(C)
        for l in range(L if dbg.get("stop") != "setup" else 0):
            xin = C.x if l == 0 else C.xres
            ph = dbg.get("phases", "abcdef")
            if "a" in ph:
                phase_a(C, l, xin)
            if dbg.get("stop") in ("a", "norm"):
                break
            if "b" in ph:
                phase_b(C, l)
            if "c" in ph:
                phase_c(C, l)
            if "d" in ph:
                phase_d(C, l)
            if "e" in ph:
                phase_e(C, l, xin, C.xmid)
            if "f" in ph:
                phase_f(C, l, C.xmid, C.y if l == L - 1 else C.xres, l == L - 1)
            if dbg.get("stop") == "l0":
                break
        P.finish()
    return nc


def phase_setup(C):
    nc, P = C.nc, C.P
    with contextlib.ExitStack() as es:
        sb = lambda n, s, d: es.enter_context(nc.sbuf_tensor("su_" + n, s, d))
        posi = sb("posi", [128, S], I32)
        posf = sb("posf", [128, S], F32)
        ang = sb("ang", [128, S], F32)
        u = sb("u", [128, S], F32)
        tb = sb("tb", [128, S], BF16)
        rc = sb("rc", [128, 4], F32)
        P.dma("sp", posi[:], C.pos.broadcast_to([128, S]))
        P.dma("sp", rc[:], C.rotc)
        P.copy("dve", posf[:], posi[:])
        two_pi = 2.0 * math.pi
        ki = sb("ki", [128, S], I32)
        kf = sb("kf", [128, S], F32)

        def sin_of(dst, a):
            P.ts("dve", u[:], a, 1.0 / two_pi, None, ALU.mult)
            P.copy("dve", ki[:], u[:])
            P.copy("dve", kf[:], ki[:])
            P.stt("dve", u[:], kf[:], -two_pi, a, ALU.mult, ALU.add)
            P.ts("dve", kf[:], u[:], math.pi, None, ALU.is_gt)
            P.stt("dve", u[:], kf[:], -two_pi, u[:], ALU.mult, ALU.add)
            P.ts("dve", kf[:], u[:], -math.pi, None, ALU.is_lt)
            P.stt("dve", u[:], kf[:], two_pi, u[:], ALU.mult, ALU.add)
            P.act(dst, u[:], AF.Sin)

        for ti, col in ((0, 0), (1, 2)):
            P.ts("dve", ang[:], posf[:], rc[:, col:col + 1], None, ALU.mult)
            P.ts("dve", posi[:].bitcast(F32), ang[:], 0.5 * math.pi, None, ALU.add)
            sin_of(tb[:], posi[:].bitcast(F32))
            P.dma("sp", C.tabs[2 * ti], tb[:])
            sin_of(kf[:], ang[:])
            P.ts("dve", tb[:], kf[:], rc[:, col + 1:col + 2], None, ALU.mult)
            P.dma("sp", C.tabs[2 * ti + 1], tb[:])
        P.flush()


def norm_T(C, l, xin, hT, gcol, es_name):
    nc, P = C.nc, C.P
    with contextlib.ExitStack() as es:
        sb = lambda n, s, d: es.enter_context(nc.sbuf_tensor(es_name + n, s, d))
        xt = [sb("xt%d" % i, [128, D], F32) for i in range(3)]
        junk = sb("junk", [128, D], BF16)
        xs = [sb("xs%d" % i, [128, D], BF16) for i in range(3)]
        ss = sb("ss", [128, NT], F32)
        sq = sb("sq", [128, NT], F32)
        rstd = sb("rstd", [128, NT], F32)
        pst = [es.enter_context(nc.psum_tensor(es_name + "pst%d" % i, [128, D], BF16)) for i in range(2)]
        def stage1(i):
            x_t = xt[i % 3]
            P.dma("sp", x_t[:], xin[i * 128:(i + 1) * 128, :])
            P.act(junk[:], x_t[:], AF.Square, accum_out=ss[:, i:i + 1])
            P.act(sq[:, i:i + 1], ss[:, i:i + 1], AF.Sqrt, bias=C.epsb[:], scale=1.0 / D)
            P.recip(rstd[:, i:i + 1], sq[:, i:i + 1])
            x_s = xs[i % 3]
            P.ts("dve", x_s[:], x_t[:], rstd[:, i:i + 1], None, ALU.mult)
            ps = pst[i % 2]
            for c in range(8):
                P.tr(ps[:, c * 128:(c + 1) * 128], x_s[:, c * 128:(c + 1) * 128], C.identb[:])

        def stage2(i):
            ps = pst[i % 2]
            for c in range(8):
                o = hT[:, c, i * 128:(i + 1) * 128]
                g = C.vp[:, gcol + c:gcol + c + 1]
                if i % 2 == 0:
                    P.amul(o, ps[:, c * 128:(c + 1) * 128], g)
                else:
                    P.ts("dve", o, ps[:, c * 128:(c + 1) * 128], g, None, ALU.mult)

        for i in range(NT + 1):
            if i < NT:
                stage1(i)
            if i >= 1:
                stage2(i - 1)
        P.flush()


def phase_a(C, l, xin):
    nc, P = C.nc, C.P
    with contextlib.ExitStack() as es:
        sb = lambda n, s, d: es.enter_context(nc.sbuf_tensor("a%d_" % l + n, s, d))
        ps = lambda n, s, d: es.enter_context(nc.psum_tensor("a%d_" % l + n, s, d))
        C.vp = sb("vp", [128, NVEC], F32)
        C.epsb = sb("epsb", [128, 1], F32)
        identf = sb("identf", [128, 128], F32)
        C.identb = sb("identb", [128, 128], BF16)
        hT = sb("hT", [128, 8, S], BF16)
        P.dma("sp", C.vp[:], C.vecs[l])
        P.dma("sp", identf[:], C.identf)
        P.memset("dve", C.epsb[:], EPS)
        P.copy("dve", C.identb[:], identf[:])
        norm_T(C, l, xin, hT, V_MIXN, "a%dn_" % l)
        if C.dbg.get("stop") == "norm":
            P.flush()
            return

        cosn = sb("cosn", [128, S], BF16)
        sinn = sb("sinn", [128, S], BF16)
        cosm = sb("cosm", [32, S], BF16)
        sinm = sb("sinm", [32, S], BF16)
        P.dma("sp", cosn[:], C.tabs[0])
        P.dma("sp", sinn[:], C.tabs[1])
        P.dma("sp", cosm[:], C.tabs[2, 0:32, :])
        P.dma("sp", sinm[:], C.tabs[3, 0:32, :])

        wst = [sb("wst%d" % i, [128, 8, 512], F32) for i in range(2)]
        wbf = [sb("wbf%d" % i, [128, 8, 512], BF16) for i in range(2)]
        pA = [ps("pA%d" % i, [128, 512], F32) for i in range(3)]
        pB = [ps("pB%d" % i, [128, 512], F32) for i in range(2)]
        pT = ps("pT", [128, 1024], BF16)
        ob = [sb("ob%d" % i, [128, 512], BF16) for i in range(4)]
        of = [sb("of%d" % i, [128, 512], F32) for i in range(2)]
        t1 = [sb("t1%d" % i, [128, 512], F32) for i in range(2)]
        t2 = [sb("t2%d" % i, [128, 512], F32) for i in range(2)]
        mss = sb("mss", [128, 4], F32)
        mo = [sb("mo%d" % i, [128, 512], BF16) for i in range(4)]
        cnt = {"a": 0, "b": 0, "o": 0, "f": 0, "t": 0, "e": 0}

        loaded = {}

        def load_group(gi):
            if gi not in loaded:
                loaded[gi] = load_group_(gi)
            if gi + 1 < len(WIN_GROUPS) and gi + 1 not in loaded:
                loaded[gi + 1] = load_group_(gi + 1)
            return loaded[gi]

        def load_group_(gi):
            n = len(WIN_GROUPS[gi])
            w_s, w_b = wst[gi % 2], wbf[gi % 2]
            src = C.wext[l, :, WIN_GOFF[gi]:WIN_GOFF[gi] + n].rearrange("(c p) n -> p c n", p=128)
            P.dma("sp", w_s[:, :, 0:n], src)
            for c in range(8):
                P.copy(("pool", "dve", "pool", "act")[c % 4], w_b[:, c, 0:n], w_s[:, c, 0:n])
            return w_b

        def fm_mm(w_b, c0, m, b, pt):
            for c in range(8):
                P.mm(pt[0:m, :], w_b[:, c, c0:c0 + m], hT[:, c, b * 512:(b + 1) * 512], start=(c == 0), stop=(c == 7))

        def evac_eng():
            cnt["e"] += 1
            return "act" if cnt["e"] % 2 else "dve"

        def fm_copy(w_b, c0, m, dst_fn, func=None, dt=BF16):
            for b in range(NB):
                pt = pA[cnt["a"] % 3]
                cnt["a"] += 1
                fm_mm(w_b, c0, m, b, pt)
                if dt == BF16:
                    o = ob[cnt["o"] % 4]
                    cnt["o"] += 1
                else:
                    o = of[cnt["f"] % 2]
                    cnt["f"] += 1
                if func is not None:
                    P.act(o[0:m, :], pt[0:m, :], func)
                else:
                    P.copy(evac_eng(), o[0:m, :], pt[0:m, :])
                for (dst, r0, r1) in dst_fn(b):
                    P.dma("sp", dst, o[r0:r1, :])

        def fm_rot(w_b, ca, cb, m, cost, sint, dst_fn, plain_fn=None):
            for b in range(NB):
                pa = pA[cnt["a"] % 3]
                cnt["a"] += 1
                pb = pB[cnt["b"] % 2]
                cnt["b"] += 1
                fm_mm(w_b, ca, m, b, pa)
                fm_mm(w_b, cb, m, b, pb)
                bs = slice(b * 512, (b + 1) * 512)
                a1, a2 = t1[cnt["t"] % 2], t2[cnt["t"] % 2]
                cnt["t"] += 1
                P.tt("dve", a1[0:m, :], pb[0:m, :], sint[0:m, bs], ALU.mult)
                P.tt("dve", a2[0:m, :], pa[0:m, :], cost[0:m, bs], ALU.mult)
                o = ob[cnt["o"] % 4]
                cnt["o"] += 1
                P.tt("pool", o[0:m, :], a1[0:m, :], a2[0:m, :], ALU.add)
                for (dst, r0, r1) in dst_fn(b):
                    P.dma("sp", dst, o[r0:r1, :])
                if plain_fn is not None:
                    o2 = ob[cnt["o"] % 4]
                    cnt["o"] += 1
                    P.copy("act", o2[0:m, :], pa[0:m, :])
                    for (dst, r0, r1) in plain_fn(b):
                        P.dma("sp", dst, o2[r0:r1, :])

        def rows(dst, r0, n=128):
            return lambda b: [(dst[r0:r0 + n, b * 512:(b + 1) * 512], 0, n)]

        gi = 0
        for p in range(4):
            w_b = load_group(gi)
            fm_rot(w_b, 0, 128, 128, cosn, sinn, rows(C.nqr, 128 * p), rows(C.nq, 128 * p))
            gi += 1
        w_b = load_group(gi)
        fm_copy(w_b, 0, 128, rows(C.nkc, 0))
        fm_copy(w_b, 128, 128, rows(C.nvc, 0))
        fm_rot(w_b, 256, 384, 128, cosn, sinn, rows(C.nks, 0))
        gi += 1
        w_b = load_group(gi)
        fm_rot(w_b, 0, 128, 128, cosn, sinn, rows(C.nkw, 0))
        fm_rot(w_b, 256, 288, 32, cosm, sinm, rows(C.mkr, 0, 32))
        fm_copy(w_b, 320, 8, lambda b: [(C.flog[:, b * 512:(b + 1) * 512], 0, 8)], dt=F32)
        gi += 1
        for dst in (C.fq, C.fk):
            w_b = load_group(gi)
            for p in range(4):
                fm_copy(w_b, 128 * p, 128,
                        (lambda p, dst: lambda b: [(dst[2 * p, 0:64, b * 512:(b + 1) * 512], 0, 64),
                                                  (dst[2 * p + 1, 0:64, b * 512:(b + 1) * 512], 64, 128)])(p, dst))
            gi += 1
        for i in range(6):
            w_b = load_group(gi)
            for p in range(4):
                fm_copy(w_b, 128 * p, 128, rows(C.gm, 512 * i + 128 * p), func=AF.Sigmoid)
            gi += 1

        def tm_mm(w_b, c0, n, i, pt):
            for c in range(8):
                P.mm(pt[:, 0:n], hT[:, c, i * 128:(i + 1) * 128], w_b[:, c, c0:c0 + n], start=(c == 0), stop=(c == 7))

        w_b = load_group(gi)
        fm_copy(w_b, 256, 24, lambda b: [(C.ngT[:, b * 512:(b + 1) * 512], 0, 24)], func=AF.Sigmoid, dt=F32)
        for i in range(NT):
            pt = pA[cnt["a"] % 3]
            cnt["a"] += 1
            tm_mm(w_b, 0, 280, i, pt)
            o = ob[cnt["o"] % 4]
            cnt["o"] += 1
            P.copy(evac_eng(), o[:, 0:256], pt[:, 0:256])
            o2 = of[cnt["f"] % 2]
            cnt["f"] += 1
            P.act(o2[:, 0:24], pt[:, 256:280], AF.Sigmoid)
            tsl = slice(i * 128, (i + 1) * 128)
            P.dma("sp", C.nvs[tsl, :], o[:, 0:128])
            P.dma("sp", C.nvw[tsl, :], o[:, 128:256])
            P.dma("sp", C.ng[tsl, :], o2[:, 0:24])
        gi += 1
        w_b = load_group(gi)
        for i in range(NT):
            pt = pA[cnt["a"] % 3]
            cnt["a"] += 1
            tm_mm(w_b, 0, 512, i, pt)
            o = ob[cnt["o"] % 4]
            cnt["o"] += 1
            P.copy(evac_eng(), o[:, :], pt[:, :])
            P.dma("sp", C.fv[i * 128:(i + 1) * 128, :], o[:, :])
        gi += 1
        for (n, dst) in ((384, C.mcq), (256, C.mckv)):
            w_b = load_group(gi)
            nch = n // 128
            pend = []

            def tail(i, o, n=n, nch=nch, dst=dst):
                for c in range(nch):
                    P.tr(pT[:, c * 128:(c + 1) * 128], o[:, c * 128:(c + 1) * 128], C.identb[:])
                o3 = ob[cnt["o"] % 4]
                cnt["o"] += 1
                P.copy("act", o3[:, 0:n], pT[:, 0:n])
                for c in range(nch):
                    P.dma("sp", dst[c * 128:(c + 1) * 128, i * 128:(i + 1) * 128], o3[:, c * 128:(c + 1) * 128])

            for i in range(NT):
                pt = pA[cnt["a"] % 3]
                cnt["a"] += 1
                tm_mm(w_b, 0, n, i, pt)
                o2 = of[cnt["f"] % 2]
                cnt["f"] += 1
                j = i % 4
                P.act(o2[:, 0:n], pt[:, 0:n], AF.Square, accum_out=mss[:, j:j + 1])
                P.act(mss[:, j:j + 1], mss[:, j:j + 1], AF.Sqrt, bias=C.epsb[:], scale=1.0 / n)
                P.recip(mss[:, j:j + 1], mss[:, j:j + 1])
                o = mo[i % 4]
                P.ts("dve", o[:, 0:n], pt[:, 0:n], mss[:, j:j + 1], None, ALU.mult)
                pend.append((i, o))
                if len(pend) > 2:
                    tail(*pend.pop(0))
            while pend:
                tail(*pend.pop(0))
            gi += 1
        P.flush()


class Pipe:
    def __init__(self, la):
        self.LA = la
        self.fifo = []
        self.tick = 0

    def pair(self, qk_fn, pv_fn):
        qk_fn()
        self.tick += 1
        self.fifo.append((self.tick + self.LA, pv_fn))
        self.pump()

    def defer(self, fn, delay):
        self.fifo.append((self.tick + delay, fn))

    def pump(self):
        i = 0
        while i < len(self.fifo):
            if self.fifo[i][0] <= self.tick:
                self.fifo.pop(i)[1]()
            else:
                i += 1

    def drain(self):
        while self.fifo:
            self.tick += 1
            self.pump()


class AttnBufs(Pipe):
    def __init__(self, C, es, tag, npo=2, npss=3, pbc=None):
        Pipe.__init__(self, npss - 1)
        self.npo = npo
        self.npss = npss
        nc = C.nc
        sb = lambda n, s, d: es.enter_context(nc.sbuf_tensor(tag + n, s, d))
        ps = lambda n, s, d: es.enter_context(nc.psum_tensor(tag + n, s, d))
        self.C = C
        self.pss = [ps("pss%d" % i, [128, 512], F32) for i in range(npss)]
        self.po = [ps("po%d" % i, [128, 512], F32) for i in range(npo)]
        self.pbc = ps("pbc", [128, 512], F32) if pbc is None else pbc
        self.nfill = C.dbg.get("fill", 0)
        self.pfill = ps("pfill", [128, 512], F32) if self.nfill else None
        self.pt = [sb("pt%d" % i, [128, 512], BF16) for i in range(npss)]
        self.osb = [sb("osb%d" % i, [65, 512], F32) for i in range(8)]
        self.obf = [sb("obf%d" % i, [64, 512], BF16) for i in range(2)]
        self.sel = sb("sel65", [65, 64], F32)
        C.P.memset("pool", self.sel[:], 0.0)
        C.P.memset("pool", self.sel[64:65, :], 1.0)
        self.masks = sb("masks", [128, 8, 512], BF16)
        C.P.dma("sp", self.masks[:], C.maskc)
        self.ns = 0
        self.nacc = 0
        self.nob = 0

    def score(self, kT_tile, qT_blk, c0, c1, scale, extra=()):
        P = self.C.P
        p_s, p_t = self.pss[self.ns % self.npss], self.pt[self.ns % self.npss]
        self.ns += 1
        P.mm(p_s[:, c0:c1], kT_tile, qT_blk[:, c0:c1], start=True, stop=(len(extra) == 0))
        for i, (lt, rh, e0, e1) in enumerate(extra):
            P.mm(p_s[:, e0:e1], lt, rh[:, e0:e1], start=False, stop=(i == len(extra) - 1))
        P.act(p_t[:, c0:c1], p_s[:, c0:c1], AF.Exp, scale=scale)
        for _ in range(self.nfill):
            P.mm(self.pfill[:, c0:c1], kT_tile, qT_blk[:, c0:c1], start=True, stop=True)
        return p_t

    def new_acc(self):
        k = self.nacc
        self.nacc += 1
        return self.po[k % self.npo], self.osb[k % 8]


def attn_finish(C, B, po, osb, dst, q, gate=None, addT=None):
    P = C.P
    P.copy("dve", osb[:], po[0:65, :])
    P.recip(osb[64:65, :], osb[64:65, :])
    if gate is not None:
        P.tt("dve", osb[64:65, :], osb[64:65, :], gate, ALU.mult)

    def e2():
        P.mm(B.pbc[0:64, :], B.sel[:], osb[:], start=True, stop=True)
        if dst is None:
            P.tt("dve", osb[0:64, :], osb[0:64, :], B.pbc[0:64, :], ALU.mult)
            return
        ob = B.obf[B.nob % 2]
        B.nob += 1
        if addT is None:
            P.tt("dve", ob[:], osb[0:64, :], B.pbc[0:64, :], ALU.mult)
        else:
            P.tt("dve", osb[0:64, :], osb[0:64, :], B.pbc[0:64, :], ALU.mult)
            for a in addT[:-1]:
                P.tt("pool", osb[0:64, :], osb[0:64, :], a, ALU.add)
            P.tt("pool", ob[:], osb[0:64, :], addT[-1], ALU.add)
        P.dma("sp", dst[:, q * 512:(q + 1) * 512], ob[:])
    B.defer(e2, 14)


def dense_causal(C, B, qT, kT, kr, vaug, scale, dst_rows):
    P = C.P
    for q in range(NB):
        po, osb = B.new_acc()
        nk = 4 * q + 4
        for kt in range(nk):
            m = kt - 4 * q
            c0 = 128 * max(m, 0)

            def qk(kt=kt, m=m, c0=c0, q=q):
                extra = [(C.identb[:], B.masks[:, m, :], 128 * m, 128 * m + 128)] if m >= 0 else []
                return B.score(kT[0:kr, kt * 128:(kt + 1) * 128], qT[0:kr, q * 512:(q + 1) * 512], c0, 512, scale, extra)
            holder = {}

            def qk_fn(qk=qk, holder=holder):
                holder["pt"] = qk()

            def pv_fn(kt=kt, c0=c0, q=q, nk=nk, holder=holder, po=po, osb=osb):
                P.mm(po[0:vaug.shape[-1], c0:512], vaug[:, kt, :], holder["pt"][:, c0:512], start=(kt == 0), stop=(kt == nk - 1))
                if kt == nk - 1:
                    attn_finish(C, B, po, osb, dst_rows, q)
            B.pair(qk_fn, pv_fn)


def fox_decay(C, l, sb, vp):
    nc, P = C.nc, C.P
    a = sb("a", [8, S], F32)
    b = sb("b", [8, S], F32)
    c = sb("c", [8, S], F32)
    hi = sb("hi", [8, S], BF16)
    lo = sb("lo", [8, S], BF16)
    on = sb("on", [8, S], BF16)
    P.dma("sp", a[:], C.flog)
    P.ts("dve", a[:], a[:], vp[0:8, V_BF:V_BF + 1], None, ALU.add)
    P.act(b[:], a[:], AF.Exp, scale=-1.0)
    P.act(a[:], b[:], AF.Ln, bias=1.0, scale=1.0)
    P.memset("dve", b[:], 1.0)
    P.add("dve", lambda e: e.tensor_tensor_scan(c[:], b[:], a[:], 0.0, ALU.mult, ALU.add), [a[:], b[:]], [c[:]])
    P.ts("dve", c[:], c[:], 8.0, None, ALU.mult)
    P.copy("dve", hi[:], c[:])
    P.copy("dve", b[:], hi[:])
    P.tt("dve", a[:], c[:], b[:], ALU.subtract)
    P.copy("dve", lo[:], a[:])
    P.memset("dve", on[:], 1.0)
    P.dma("sp", C.fk[:, 66, :], hi[:])
    P.dma("sp", C.fk[:, 67, :], lo[:])
    P.dma("sp", C.fk[:, 64, :], on[:])
    P.dma("sp", C.fk[:, 65, :], on[:])
    P.dma("sp", C.fq[:, 66, :], on[:])
    P.dma("sp", C.fq[:, 67, :], on[:])
    P.ts("dve", hi[:], hi[:], -1.0, None, ALU.mult)
    P.ts("dve", lo[:], lo[:], -1.0, None, ALU.mult)
    P.dma("sp", C.fq[:, 64, :], hi[:])
    P.dma("sp", C.fq[:, 65, :], lo[:])


def phase_c(C, l):
    nc, P = C.nc, C.P
    with contextlib.ExitStack() as es:
        sb = lambda n, s, d: es.enter_context(nc.sbuf_tensor("c%d_" % l + n, s, d))
        identf = sb("identf", [128, 128], F32)
        C.identb = sb("identb", [128, 128], BF16)
        P.dma("sp", identf[:], C.identf)
        P.copy("dve", C.identb[:], identf[:])
        B = AttnBufs(C, es, "c%d_" % l, npss=5)
        qT = [sb("qT%d" % i, [128, S], BF16) for i in range(2)]
        kT = [sb("kT%d" % i, [128, S], BF16) for i in range(2)]
        va = [sb("va%d" % i, [128, NT, 128], BF16) for i in range(2)]
        for i in range(2):
            P.memset("pool", va[i][:, :, 64:128], 0.0)
            P.memset("pool", va[i][:, :, 64:65], 1.0)
            P.memset("pool", qT[i][64:128, :], 0.0)
            P.memset("pool", kT[i][64:128, :], 0.0)
        def loadh(h):
            k = h % 2
            P.dma("sp", qT[k][0:68, :], C.fq[h])
            P.dma("sp", kT[k][0:68, :], C.fk[h])
            P.dma("sp", va[k][:, :, 0:64], C.fv[:, h * 64:(h + 1) * 64].rearrange("(n p) d -> p n d", p=128))
        loadh(0)
        for h in range(8):
            k = h % 2
            if h + 1 < 8:
                B.drain()
                loadh(h + 1)
            dense_causal(C, B, qT[k], kT[k], 128, va[k], 0.125, C.oT[1][h * 64:(h + 1) * 64, :])
        B.drain()
        P.flush()


def phase_d(C, l):
    nc, P = C.nc, C.P
    with contextlib.ExitStack() as es:
        sb = lambda n, s, d: es.enter_context(nc.sbuf_tensor("d%d_" % l + n, s, d))
        ps = lambda n, s, d: es.enter_context(nc.psum_tensor("d%d_" % l + n, s, d))
        vp = sb("vp", [128, NVEC], F32)
        P.dma("sp", vp[:], C.vecs[l])
        identf = sb("identf", [128, 128], F32)
        C.identb = sb("identb", [128, 128], BF16)
        P.dma("sp", identf[:], C.identf)
        P.copy("dve", C.identb[:], identf[:])
        cq = sb("cq", [128, 3, S], BF16)
        ckv = sb("ckv", [128, 2, S], BF16)
        P.dma("sp", cq[:], C.mcq.rearrange("(c p) n -> p c n", p=128))
        P.dma("sp", ckv[:], C.mckv.rearrange("(c p) n -> p c n", p=128))
        cosm = sb("cosm", [128, S], BF16)
        sinm = sb("sinm", [128, S], BF16)
        P.dma("sp", cosm[:], C.tabs[2])
        P.dma("sp", sinm[:], C.tabs[3])
        wuq = sb("wuq", [128, 3, 1536], BF16)
        wukv = sb("wukv", [128, 2, 1024], BF16)
        va = sb("va", [128, NT, 8 * 65 + 63], BF16)
        stg = [sb("stg%d" % i, [128, 1536], F32) for i in range(2)]
        for c in range(3):
            P.dma("sp", stg[c % 2][:], C.wuqx[l, c * 128:(c + 1) * 128, :])
            P.ts(("pool", "dve")[c % 2], wuq[:, c, :], stg[c % 2][:], vp[:, V_QN + c:V_QN + c + 1], None, ALU.mult)
        for c in range(2):
            P.dma("sp", stg[(c + 1) % 2][:, 0:1024], C.wukvx[l, c * 128:(c + 1) * 128, :])
            P.ts(("dve", "pool")[c % 2], wukv[:, c, :], stg[(c + 1) % 2][:, 0:1024], vp[:, V_KVN + c:V_KVN + c + 1], None, ALU.mult)
        pA = [ps("pA%d" % i, [128, 512], F32) for i in range(2)]
        B = AttnBufs(C, es, "d%d_" % l, npss=4, pbc=pA[1])
        pK = pA[0]
        P.memset("pool", va[:, :, 520:583], 0.0)
        vav = va[:, :, 0:520].rearrange("p n (h d) -> p n h d", h=8)
        P.memset("pool", vav[:, :, :, 64:65], 1.0)
        for i in range(NT):
            pp = pA[i % 2]
            for c in range(2):
                P.mm(pp[:], ckv[:, c, i * 128:(i + 1) * 128], wukv[:, c, 512:1024], start=(c == 0), stop=(c == 1))
            P.copy("act" if i % 2 else "dve", vav[:, i, :, 0:64], pp[:].rearrange("p (h d) -> p h d", h=8))
        qT = [sb("qT%d" % i, [128, S], BF16) for i in range(2)]
        kT = [sb("kT%d" % i, [128, S], BF16) for i in range(2)]
        for i in range(2):
            P.memset("pool", qT[i][96:128, :], 0.0)
            P.memset("pool", kT[i][96:128, :], 0.0)
        t1 = [sb("t1%d" % i, [96, 512], F32) for i in range(2)]
        t2 = [sb("t2%d" % i, [96, 512], F32) for i in range(2)]
        for i in range(2):
            P.dma("sp", kT[i][64:96, :], C.mkr)
        cntn = [0]

        def proj(h, qlist=None, part=3):
            k = h % 2
            n = cntn[0]
            for q in (range(NB) if qlist is None else qlist):
                qs = slice(q * 512, (q + 1) * 512)
                if part == 2:
                    for c in range(2):
                        P.mm(pK[0:64, :], wukv[:, c, h * 64:(h + 1) * 64], ckv[:, c, qs], start=(c == 0), stop=(c == 1))
                    P.copy("dve", kT[k][0:64, qs], pK[0:64, :])
                    continue
                pa, pb = pA[0], pA[1]
                for c in range(3):
                    P.mm(pa[0:96, :], wuq[:, c, h * 192:h * 192 + 96], cq[:, c, qs], start=(c == 0), stop=(c == 2))
                for c in range(3):
                    P.mm(pb[0:96, :], wuq[:, c, h * 192 + 96:h * 192 + 192], cq[:, c, qs], start=(c == 0), stop=(c == 2))
                a1, a2 = t1[n % 2], t2[n % 2]
                n += 1
                cntn[0] = n
                P.tt("dve", a1[64:96, :], pb[64:96, :], sinm[64:96, qs], ALU.mult)
                P.tt("dve", a2[64:96, :], pa[64:96, :], cosm[64:96, qs], ALU.mult)
                P.copy("dve", qT[k][0:64, qs], pa[0:64, :])
                P.tt("pool", qT[k][64:96, qs], a1[64:96, :], a2[64:96, :], ALU.add)
                if part == 1:
                    continue
                pk = pK
                for c in range(2):
                    P.mm(pk[0:64, :], wukv[:, c, h * 64:(h + 1) * 64], ckv[:, c, qs], start=(c == 0), stop=(c == 1))
                P.copy("dve", kT[k][0:64, qs], pk[0:64, :])

        proj(0)
        for h in range(8):
            k = h % 2
            if h + 1 < 8:
                for q in range(NB):
                    B.defer((lambda h=h, q=q: proj(h + 1, [q], 1)), 3 + 14 * q)
                    B.defer((lambda h=h, q=q: proj(h + 1, [q], 2)), 10 + 14 * q)
            dense_causal(C, B, qT[k], kT[k], 128, va[:, :, h * 65:h * 65 + 128], 96 ** -0.5, C.oT[2][h * 64:(h + 1) * 64, :])
        B.drain()
        P.flush()


def phase_b(C, l):
    nc, P = C.nc, C.P
    GC = 0.7978845608028654
    with contextlib.ExitStack() as es:
        sb = lambda n, s, d: es.enter_context(nc.sbuf_tensor("b%d_" % l + n, s, d))
        ps = lambda n, s, d: es.enter_context(nc.psum_tensor("b%d_" % l + n, s, d))
        vp = sb("vp", [128, NVEC], F32)
        P.dma("sp", vp[:], C.vecs[l])
        identf = sb("identf", [128, 128], F32)
        C.identb = sb("identb", [128, 128], BF16)
        P.dma("sp", identf[:], C.identf)
        P.copy("dve", C.identb[:], identf[:])
        kcT = [sb("kcT%d" % g, [64, 256], BF16) for g in range(2)]
        vca = [sb("vca%d" % g, [128, 2, 129], BF16) for g in range(2)]
        selq = [sb("selq%d" % i, [64, 512], BF16) for i in range(2)]
        gsb = sb("gsb", [128, NT, 24], F32)
        P.dma("sp", gsb[:], C.ng.rearrange("(n p) c -> p n c", p=128))
        with contextlib.ExitStack() as es1:
            sb1 = lambda n, s, d: es1.enter_context(nc.sbuf_tensor("b%d1_" % l + n, s, d))
            w1s = sb1("w1s", [64, 32, 256], F32)
            w1 = [sb1("w1%d" % i, [64, 32, 256], BF16) for i in range(2)]
            w2s = sb1("w2s", [128, 2, 64], F32)
            w2 = [sb1("w2%d" % i, [128, 2, 64], BF16) for i in range(2)]
            ovs = sb1("ovs", [128, 2, 64], F32)
            srcs = [sb1("src%d" % i, [64, S], BF16) for i in range(2)]
            peb = [sb1("peb%d" % i, [64, 32], BF16) for i in range(2)]
            cb = sb1("cb", [128, 4], F32)
            hx = sb1("hx", [128, 256], F32)
            h2 = sb1("h2", [128, 256], F32)
            h3 = sb1("h3", [128, 256], F32)
            gl = sb1("gl", [128, 2, 256], BF16)
            ps1 = lambda n, s, d: es1.enter_context(nc.psum_tensor("b%d1_" % l + n, s, d))
            ph = [ps1("ph%d" % i, [128, 512], F32) for i in range(2)]
            pk = ps1("pk", [128, 512], F32)
            pcb = ps1("pcb", [128, 512], F32)
            for kv, (w1d, w2d) in enumerate(((C.cw1k, C.cw2k), (C.cw1v, C.cw2v))):
                P.dma("sp", w1s[:], w1d[l].rearrange("(l d) n -> d l n", d=64))
                for q4 in range(4):
                    P.copy(("pool", "dve", "act", "pool")[q4], w1[kv][:, q4 * 8:(q4 + 1) * 8, :], w1s[:, q4 * 8:(q4 + 1) * 8, :])
                P.dma("sp", w2s[:], w2d[l].rearrange("(c p) n -> p c n", p=128))
                P.copy("dve", w2[kv][:], w2s[:])
                P.copy("dve", peb[kv][:], vp[0:64, (V_PEK if kv == 0 else V_PEV):(V_PEK if kv == 0 else V_PEV) + 32])
            P.dma("sp", ovs[:], C.ovl.rearrange("(c p) n -> p c n", p=128))
            P.memset("pool", gl[:], 0.0)
            for g in range(2):
                P.memset("pool", kcT[g][:], 0.0)
                P.copy("pool", vca[g][:, :, 65:129], ovs[:])
                P.memset("pool", vca[g][:, :, 64:65], 1.0)
            for kv in range(2):
                for hc in range(2):
                    i = kv * 2 + hc
                    for li in range(32):
                        P.mm(pcb[:, i:i + 1], w1[kv][:, li, hc * 128:(hc + 1) * 128], peb[kv][:, li:li + 1], start=(li == 0), stop=(li == 31))
            P.copy("dve", cb[:], pcb[:, 0:4])
            nsrc = 0
            for kv in range(2):
                for g in range(2):
                    src = srcs[nsrc % 2]
                    nsrc += 1
                    P.dma("sp", src[:], (C.nkc if kv == 0 else C.nvc)[g * 64:(g + 1) * 64, :])
                    for hc in range(2):
                        for li in range(32):
                            P.mm(ph[hc][:, 0:255], w1[kv][:, li, hc * 128:(hc + 1) * 128], src[:, li:li + 16 * 254 + 1:16], start=(li == 0), stop=(li == 31))
                        P.ts("dve", hx[:, 0:255], ph[hc][:, 0:255], cb[:, kv * 2 + hc:kv * 2 + hc + 1], None, ALU.add)
                        P.act(h2[:, 0:255], hx[:, 0:255], AF.Square)
                        P.ts("dve", h2[:, 0:255], h2[:, 0:255], 0.044715, 1.0, ALU.mult, ALU.add)
                        P.tt("dve", h2[:, 0:255], h2[:, 0:255], hx[:, 0:255], ALU.mult)
                        P.act(h3[:, 0:255], h2[:, 0:255], AF.Tanh, scale=GC)
                        P.ts("dve", h3[:, 0:255], h3[:, 0:255], 1.0, 0.5, ALU.add, ALU.mult)
                        P.tt("dve", gl[:, hc, 0:255], h3[:, 0:255], hx[:, 0:255], ALU.mult)
                    if kv == 0:
                        for hc in range(2):
                            P.mm(pk[0:64, 0:255], w2[0][:, hc, :], gl[:, hc, 0:255], start=(hc == 0), stop=(hc == 1))
                        P.copy("dve", kcT[g][:, 0:255], pk[0:64, 0:255])
                    else:
                        for c in range(2):
                            for hc in range(2):
                                P.mm(pk[:, c * 64:(c + 1) * 64], gl[:, hc, c * 128:(c + 1) * 128], w2[1][:, hc, :],
                                     start=(hc == 0 and c == 0), stop=(hc == 1 and c == 1))
                        P.copy("dve", vca[g][:, :, 0:64], pk[:, 0:128].rearrange("p (c d) -> p c d", c=2))
            fox_decay(C, l, lambda n, s_, d: sb1("fx" + n, s_, d), vp)
            P.flush()
        with contextlib.ExitStack() as es1:
            sb1 = lambda n, s, d: es1.enter_context(nc.sbuf_tensor("b%d2_" % l + n, s, d))
            cmk = sb1("cmk", [128, 2, S], BF16)
            P.dma("sp", cmk[:], C.cmpm)
            fmul = sb1("fmul", [128, NT, 64], F32)
            fadd = sb1("fadd", [128, NT, 64], F32)
            P.dma("sp", fmul[:], C.forc[0].rearrange("p (n j) -> p n j", j=64))
            P.dma("sp", fadd[:], C.forc[1].rearrange("p (n j) -> p n j", j=64))
            qg = [sb1("qg%d" % i, [64, S], BF16) for i in range(4)]
            ps1 = lambda n, s, d: es1.enter_context(nc.psum_tensor("b%d2_" % l + n, s, d))
            pss = [ps1("pss%d" % i, [128, 512], F32) for i in range(2)]
            pcxs = [[ps1("pcx%d_%d" % (k, i), [128, 512], F32) for i in range(2)] for k in range(2)]
            pT = ps1("pT2", [128, 1024], BF16)
            pTf = ps1("pTf", [128, 512], F32)
            ocT = [sb1("ocT%d" % i, [64, 512], F32) for i in range(2)]
            pt = [sb1("pt%d" % i, [128, 512], BF16) for i in range(2)]
            impacc = sb1("impacc", [128, 4, 64], F32)
            imptmp = sb1("imptmp", [128, 4, 64], F32)
            oc = [sb1("oc%d" % i, [128, 4, 64], F32) for i in range(2)]
            impm = sb1("impm", [128, 4, 64], F32)
            imp2 = sb1("imp2", [128, 4, 64], F32)
            t8 = sb1("t8", [128, 4, 16], F32)
            selb = sb1("selb", [128, 4, 64], BF16)
            pipe = Pipe(1)
            st = {"nsx": 0, "noc": 0}
            smb = [sb1("smb%d" % i, [128, 16], F32) for i in range(2)]
            for g in range(2):
                for r in range(4):
                    P.dma("sp", qg[r][:], C.nq[(4 * g + r) * 64:(4 * g + r + 1) * 64, :])
                for q in range(NB):
                    qs = slice(q * 512, (q + 1) * 512)
                    nch = 1 if q < 4 else 2
                    for r in range(4):
                        h = 4 * g + r
                        pset = pcxs[(st["noc"]) % 2]
                        o_c = oc[st["noc"] % 2]
                        o_t = ocT[st["noc"] % 2]
                        sm = smb[st["noc"] % 2]
                        st["noc"] += 1
                        for c in range(nch):
                            holder = {}

                            def qk_fn(c=c, g=g, r=r, qs=qs, holder=holder):
                                p_s, p_t = pss[st["nsx"] % 2], pt[st["nsx"] % 2]
                                st["nsx"] += 1
                                P.mm(p_s[:], kcT[g][:, c * 128:(c + 1) * 128], qg[r][:, qs], start=True, stop=False)
                                P.mm(p_s[:], C.identb[:], cmk[:, c, qs], start=False, stop=True)
                                P.act(p_t[:], p_s[:], AF.Exp, scale=0.125)
                                holder["pt"] = p_t

                            def pv_fn(c=c, g=g, r=r, h=h, q=q, qs=qs, nch=nch, holder=holder, pset=pset, o_c=o_c, o_t=o_t, sm=sm):
                                p_t = holder["pt"]
                                for j in range(4):
                                    P.mm(pset[j // 2][:, (j % 2) * 129:(j % 2) * 129 + 129], p_t[:, j * 128:(j + 1) * 128], vca[g][:, c, :],
                                         start=(c == 0 and j % 2 == 0), stop=(c == nch - 1 and j % 2 == 1))
                                if c != nch - 1:
                                    return
                                for b in range(2):
                                    pc3 = pset[b][:, 0:258].rearrange("p (j c) -> p j c", j=2)
                                    den2 = pset[b][:, 64:194:129]
                                    j0 = 2 * b
                                    tt0 = 4 * q + j0
                                    P.ts("dve", sm[:, j0:j0 + 2], den2, 1e-30, None, ALU.max)
                                    P.recip(sm[:, 4 + j0:6 + j0], sm[:, j0:j0 + 2])
                                    P.tt("dve", sm[:, 8 + j0:10 + j0], sm[:, 4 + j0:6 + j0], gsb[:, tt0:tt0 + 2, h * 3], ALU.mult)
                                    P.tt("dve", o_c[:, j0:j0 + 2, :], pc3[:, :, 0:64],
                                         sm[:, 8 + j0:10 + j0].unsqueeze(2).broadcast_to([128, 2, 64]), ALU.mult)
                                    rbc = sm[:, 4 + j0:6 + j0].unsqueeze(2).broadcast_to([128, 2, 64])
                                    if r == 0:
                                        P.tt("dve", impacc[:, j0:j0 + 2, :], pc3[:, :, 65:129], rbc, ALU.mult)
                                    else:
                                        P.tt("dve", imptmp[:, j0:j0 + 2, :], pc3[:, :, 65:129], rbc, ALU.mult)
                                        P.tt("pool", impacc[:, j0:j0 + 2, :], impacc[:, j0:j0 + 2, :], imptmp[:, j0:j0 + 2, :], ALU.add)

                                def e2():
                                    for j in range(4):
                                        P.tr(pTf[0:64, j * 128:(j + 1) * 128], o_c[:, j, :], identf[:])
                                    P.copy("act", o_t[:], pTf[0:64, :])
                                    P.dma("sp", C.ocmpT[h * 64:(h + 1) * 64, qs], o_t[:])
                                pipe.defer(e2, 2)
                                if r != 3:
                                    return
                                P.tt("dve", impm[:], impacc[:], fmul[:, 4 * q:4 * q + 4, :], ALU.mult)
                                P.tt("dve", impm[:], impm[:], fadd[:, 4 * q:4 * q + 4, :], ALU.add)
                                for j in range(4):
                                    P.add("dve", lambda e, j=j: e.max(t8[:, j, 0:8], impm[:, j, :]), [impm[:, j, :]], [t8[:, j, 0:8]])
                                    P.add("dve", lambda e, j=j: e.match_replace(imp2[:, j, :], t8[:, j, 0:8], impm[:, j, :], -3.0e9),
                                          [t8[:, j, 0:8], impm[:, j, :]], [imp2[:, j, :]])
                                    P.add("dve", lambda e, j=j: e.max(t8[:, j, 8:16], imp2[:, j, :]), [imp2[:, j, :]], [t8[:, j, 8:16]])
                                P.tt("dve", imp2[:], impm[:], t8[:, :, 15:16].broadcast_to([128, 4, 64]), ALU.is_ge)
                                P.ts("dve", selb[:], imp2[:], -1.0, -NEG, ALU.add, ALU.mult)

                                def e3():
                                    for j in range(4):
                                        P.tr(pT[0:64, j * 128:(j + 1) * 128], selb[:, j, :], C.identb[:])
                                    P.copy("dve", selq[q % 2][:], pT[0:64, 0:512])
                                    P.dma("sp", C.selTd[g, :, qs], selq[q % 2][:])
                                pipe.defer(e3, 2)
                            pipe.pair(qk_fn, pv_fn)
            pipe.drain()
            P.flush()
        with contextlib.ExitStack() as es1:
            sb1 = lambda n, s, d: es1.enter_context(nc.sbuf_tensor("b%d3_" % l + n, s, d))
            B = AttnBufs(C, es1, "b%d3_" % l, npo=3, npss=4)
            kse = [sb1("kse%d" % g, [128, S], BF16) for g in range(2)]
            kwz = [sb1("kwz%d" % g, [128, S], BF16) for g in range(2)]
            vsa = [sb1("vsa%d" % g, [128, NT, 128], BF16) for g in range(2)]
            vwa = [sb1("vwa%d" % g, [128, NT, 128], BF16) for g in range(2)]
            for g in range(2):
                P.memset("pool", vsa[g][:, :, 64:128], 0.0)
                P.memset("pool", vwa[g][:, :, 64:128], 0.0)
                P.dma("sp", kse[g][0:64, :], C.nks[g * 64:(g + 1) * 64, :])
                P.dma("sp", kse[g][64:128, :], C.esel)
                P.dma("sp", kwz[g][0:64, :], C.nkw[g * 64:(g + 1) * 64, :])
                P.memset("pool", kwz[g][64:128, :], 0.0)
                P.dma("sp", vsa[g][:, :, 0:64], C.nvs[:, g * 64:(g + 1) * 64].rearrange("(n p) d -> p n d", p=128))
                P.dma("sp", vwa[g][:, :, 0:64], C.nvw[:, g * 64:(g + 1) * 64].rearrange("(n p) d -> p n d", p=128))
                P.memset("pool", vsa[g][:, :, 64:65], 1.0)
                P.memset("pool", vwa[g][:, :, 64:65], 1.0)
            qr = [sb1("qr%d" % i, [128, S], BF16) for i in range(2)]
            ocl = [sb1("ocl%d" % i, [64, 512], F32) for i in range(3)]
            gq = [sb1("gq%d" % i, [65, 2, 512], F32) for i in range(3)]
            nq_ = 0
            def loadq(h):
                P.dma("sp", qr[h % 2][0:64, :], C.nqr[h * 64:(h + 1) * 64, :])
                P.dma("sp", qr[h % 2][64:128, :], C.selTd[h // 4])
            loadq(0)
            for h in range(8):
                g = h // 4
                q_r = qr[h % 2]
                if h + 1 < 8:
                    loadq(h + 1)
                for q in range(NB):
                    qs = slice(q * 512, (q + 1) * 512)
                    k2 = nq_ % 3
                    nq_ += 1
                    P.dma("sp", ocl[k2][:], C.ocmpT[h * 64:(h + 1) * 64, qs])
                    P.dma("sp", gq[k2][64:65, 0, :], C.ngT[h * 3 + 1:h * 3 + 2, qs])
                    P.dma("sp", gq[k2][64:65, 1, :], C.ngT[h * 3 + 2:h * 3 + 3, qs])
                    po_s, os_ = B.new_acc()
                    po_w, ow_ = B.new_acc()
                    nk = 4 * q + 4
                    for kt in range(nk):
                        m = kt - 4 * q
                        c0 = 128 * max(m, 0)
                        holder = {}

                        def qk_fn(kt=kt, m=m, c0=c0, qs=qs, holder=holder, g=g, q_r=q_r):
                            extra = []
                            if m >= 0:
                                extra.append((C.identb[:], B.masks[:, m, :], 128 * m, 128 * m + 128))
                            holder["pt"] = B.score(kse[g][:, kt * 128:(kt + 1) * 128], q_r[:, qs], c0, 512, 0.125, extra)

                        def pv_fn(kt=kt, c0=c0, nk=nk, holder=holder, g=g, po_s=po_s, os_=os_, k2=k2, q=q):
                            P.mm(po_s[:, c0:512], vsa[g][:, kt, :], holder["pt"][:, c0:512], start=(kt == 0), stop=(kt == nk - 1))
                            if kt == nk - 1:
                                attn_finish(C, B, po_s, os_, None, q, gate=gq[k2][64:65, 0, :])
                        B.pair(qk_fn, pv_fn)
                    kts = [4 * q] + list(range(max(0, 4 * q - 4), 4 * q)) + list(range(4 * q + 1, nk))
                    k0, klast = kts[0], kts[-1]
                    for kt in kts:
                        mp = kt - (4 * q - 4)
                        if mp < 4:
                            c0, c1, mi, e0 = 0, 128 * (mp + 1), 4 + mp, 128 * mp
                        else:
                            c0, c1, mi, e0 = 128 * (mp - 4), 512, mp - 4, 128 * (mp - 4)
                        holder = {}

                        def qk_fn(kt=kt, c0=c0, c1=c1, mi=mi, e0=e0, qs=qs, holder=holder, g=g, q_r=q_r):
                            holder["pt"] = B.score(kwz[g][:, kt * 128:(kt + 1) * 128], q_r[:, qs], c0, c1, 0.125,
                                                   [(C.identb[:], B.masks[:, mi, :], e0, e0 + 128)])

                        def pv_fn(kt=kt, c0=c0, c1=c1, k0=k0, klast=klast, holder=holder, g=g, po_w=po_w, ow_=ow_, os_=os_, k2=k2, q=q, h=h):
                            P.mm(po_w[:, c0:c1], vwa[g][:, kt, :], holder["pt"][:, c0:c1], start=(kt == k0), stop=(kt == klast))
                            if kt == klast:
                                attn_finish(C, B, po_w, ow_, C.oT[0][h * 64:(h + 1) * 64, :], q, gate=gq[k2][64:65, 1, :],
                                            addT=[os_[0:64, :], ocl[k2][:]])
                        B.pair(qk_fn, pv_fn)
            B.drain()
            P.flush()


def load_cast(C, P, dst_bf, src_dram, stage, rows_chunks, ncols, eng="pool"):
    P.dma("sp", stage[:, 0:rows_chunks, 0:ncols], src_dram.rearrange("(c p) n -> p c n", p=128))
    for c in range(rows_chunks):
        P.copy(("pool", "act", "dve")[c % 3], dst_bf[:, c, 0:ncols], stage[:, c, 0:ncols])


def phase_e(C, l, xin, xout):
    nc, P = C.nc, C.P
    with contextlib.ExitStack() as es:
        sb = lambda n, s, d: es.enter_context(nc.sbuf_tensor("e%d_" % l + n, s, d))
        ps = lambda n, s, d: es.enter_context(nc.psum_tensor("e%d_" % l + n, s, d))
        stage = sb("stage", [128, 4, 1024], F32)
        wbr = [sb("wbr%d" % i, [128, 4, 1024], BF16) for i in range(3)]
        wout = sb("wout", [128, 8, 1024], BF16)
        for i in range(3):
            load_cast(C, P, wbr[i], C.wbr[l, i], stage, 4, 1024)
        for hf in range(2):
            P.dma("sp", stage[:, :, :], C.wout[l, hf * 512:(hf + 1) * 512, :].rearrange("(c p) n -> p c n", p=128))
            for c in range(4):
                P.copy(("pool", "act", "dve")[c % 3], wout[:, hf * 4 + c, :], stage[:, c, :])
        oT = [[sb("oT%d_%d" % (i, k), [128, 4, 512], BF16) for i in range(3)] for k in range(2)]
        gt = [[sb("g%d_%d" % (i, k), [128, 512], BF16) for i in range(3)] for k in range(4)]
        pbr = [[ps("pbr%d_%d" % (i, k), [128, 512], F32) for i in range(3)] for k in range(2)]
        po = [ps("po%d" % k, [128, 512], F32) for k in range(2)]
        m1 = [sb("m1_%d" % k, [128, 512], F32) for k in range(2)]
        m2 = [sb("m2_%d" % k, [128, 512], F32) for k in range(2)]
        m3 = [sb("m3_%d" % k, [128, 512], F32) for k in range(2)]
        mT = [sb("mT%d" % k, [128, 8, 512], BF16) for k in range(2)]
        xt = [sb("xt%d" % k, [128, D], F32) for k in range(2)]
        n = 0
        nx = 0

        def load_oT(q):
            for i in range(3):
                P.dma("sp", oT[q % 2][i][:], C.oT[i][:, q * 512:(q + 1) * 512].rearrange("(c p) n -> p c n", p=128))

        def load_g(q, fc):
            k4 = (q * 8 + fc) % 4
            for i in range(3):
                P.dma("sp", gt[k4][i][:], C.gm[i * 1024 + fc * 128:i * 1024 + (fc + 1) * 128, q * 512:(q + 1) * 512])

        st_e = {"nx": 0}

        def out_proj(q):
            for j in range(4):
                nx = st_e["nx"]
                x_t = xt[nx % 2]
                tsl = slice(q * 512 + j * 128, q * 512 + (j + 1) * 128)
                P.dma("sp", x_t[:], xin[tsl, :])
                for hf in range(2):
                    pp = po[hf]
                    for fc in range(8):
                        P.mm(pp[:], mT[q % 2][:, fc, j * 128:(j + 1) * 128], wout[:, fc, hf * 512:(hf + 1) * 512], start=(fc == 0), stop=(fc == 7))
                    P.tt("dve", x_t[:, hf * 512:(hf + 1) * 512], pp[:], x_t[:, hf * 512:(hf + 1) * 512], ALU.add)
                P.dma("act", xout[tsl, :], x_t[:])
                st_e["nx"] = nx + 1

        jobs = [(q, fc) for q in range(NB) for fc in range(8)]
        load_oT(0)
        load_g(*jobs[0])
        load_g(*jobs[1])
        for q in range(NB):
            qs = slice(q * 512, (q + 1) * 512)
            if q + 1 < NB:
                load_oT(q + 1)
            for fc in range(8):
                k = n % 2
                k4 = n % 4
                if n + 2 < len(jobs):
                    load_g(*jobs[n + 2])
                n += 1
                for i in range(3):
                    for c in range(4):
                        P.mm(pbr[k][i][:], wbr[i][:, c, fc * 128:(fc + 1) * 128], oT[q % 2][i][:, c, :], start=(c == 0), stop=(c == 3))
                P.tt("dve", m1[k][:], pbr[k][0][:], gt[k4][0][:], ALU.mult)
                P.tt("dve", m2[k][:], pbr[k][1][:], gt[k4][1][:], ALU.mult)
                P.tt("dve", m3[k][:], pbr[k][2][:], gt[k4][2][:], ALU.mult)
                P.tt("pool", m1[k][:], m1[k][:], m2[k][:], ALU.add)
                P.tt("pool", mT[q % 2][:, fc, :], m1[k][:], m3[k][:], ALU.add)
                if fc == 1 and q >= 1:
                    out_proj(q - 1)
        out_proj(NB - 1)
        P.flush()


def phase_f(C, l, xin, xout, final):
    nc, P = C.nc, C.P
    with contextlib.ExitStack() as es:
        sb = lambda n, s, d: es.enter_context(nc.sbuf_tensor("f%d_" % l + n, s, d))
        ps = lambda n, s, d: es.enter_context(nc.psum_tensor("f%d_" % l + n, s, d))
        C.vp = sb("vp", [128, NVEC], F32)
        C.epsb = sb("epsb", [128, 1], F32)
        identf = sb("identf", [128, 128], F32)
        C.identb = sb("identb", [128, 128], BF16)
        hT = sb("hT", [128, 8, S], BF16)
        P.dma("sp", C.vp[:], C.vecs[l])
        P.dma("sp", identf[:], C.identf)
        P.memset("dve", C.epsb[:], EPS)
        P.copy("dve", C.identb[:], identf[:])
        norm_T(C, l, xin, hT, V_FFNN, "f%dn_" % l)
        with contextlib.ExitStack() as es1:
            sb1 = lambda n, s, d: es1.enter_context(nc.sbuf_tensor("f%d_" % l + n, s, d))
            wst = [sb1("wst%d" % i, [128, 8, 512], F32) for i in range(2)]
            wbf = [sb1("wbf%d" % i, [128, 8, 512], BF16) for i in range(2)]
            pu = [ps("pu%d" % i, [128, 512], F32) for i in range(3)]
            pv = [ps("pv%d" % i, [128, 512], F32) for i in range(3)]
            U = [sb1("U%d" % i, [128, S + 2], F32) for i in range(2)]
            for i in range(2):
                P.memset("dve", U[i][:, 0:2], 0.0)
            tt_ = [sb1("t%d" % i, [128, 512], F32) for i in range(3)]
            sg = [sb1("sg%d" % i, [128, 512], F32) for i in range(3)]
            yb = [sb1("yb%d" % i, [128, 512], BF16) for i in range(3)]
            n = 0
            def loadw(g):
                w_s, w_b = wst[g % 2], wbf[g % 2]
                P.dma("sp", w_s[:], C.wupx[l, :, g * 512:(g + 1) * 512].rearrange("(c p) n -> p c n", p=128))
                for c in range(8):
                    P.copy("pool", w_b[:, c, :], w_s[:, c, :])
            loadw(0)
            for g in range(NFC // 2):
                w_b = wbf[g % 2]
                if g + 1 < NFC // 2:
                    loadw(g + 1)
                for f2 in range(2):
                    fc = 2 * g + f2
                    for q in range(NB):
                        k = n % 3
                        n += 1
                        qs = slice(q * 512, (q + 1) * 512)
                        for c in range(8):
                            P.mm(pu[k][:], w_b[:, c, f2 * 256:f2 * 256 + 128], hT[:, c, qs], start=(c == 0), stop=(c == 7))
                        for c in range(8):
                            P.mm(pv[k][:], w_b[:, c, f2 * 256 + 128:f2 * 256 + 256], hT[:, c, qs], start=(c == 0), stop=(c == 7))
                        u = U[fc % 2]
                        b0 = q * 512
                        P.copy("act", u[:, b0 + 2:b0 + 514], pu[k][:])
                        t = tt_[k]
                        P.ts("dve", t[:], u[:, b0 + 2:b0 + 514], C.vp[:, V_CW2 + fc:V_CW2 + fc + 1], C.vp[:, V_CB + fc:V_CB + fc + 1], ALU.mult, ALU.add)
                        P.stt("dve", t[:], u[:, b0 + 1:b0 + 513], C.vp[:, V_CW1 + fc:V_CW1 + fc + 1], t[:], ALU.mult, ALU.add)
                        P.stt("dve", t[:], u[:, b0:b0 + 512], C.vp[:, V_CW0 + fc:V_CW0 + fc + 1], t[:], ALU.mult, ALU.add)
                        P.act(sg[k][:], t[:], AF.Silu)
                        P.tt("dve", yb[k][:], pv[k][:], sg[k][:], ALU.mult)
                        P.dma("sp", C.yT[fc * 128:(fc + 1) * 128, qs], yb[k][:])
            P.flush()
    with contextlib.ExitStack() as es:
        sb = lambda n, s, d: es.enter_context(nc.sbuf_tensor("g%d_" % l + n, s, d))
        ps = lambda n, s, d: es.enter_context(nc.psum_tensor("g%d_" % l + n, s, d))
        stages = [sb("stage%d" % i, [128, 11, 512], F32) for i in range(2)]
        wd = sb("wd", [128, NFC, 1024], BF16)
        for hf in range(2):
            for cc in range(2):
                stage = stages[cc]
                P.dma("sp", stage[:], C.wdown[l, cc * 1408:(cc + 1) * 1408, hf * 512:(hf + 1) * 512].rearrange("(c p) n -> p c n", p=128))
                for c in range(11):
                    P.copy(("pool", "act", "dve")[c % 3], wd[:, cc * 11 + c, hf * 512:(hf + 1) * 512], stage[:, c, :])
        yT = [sb("yT%d" % k, [128, NFC, 512], BF16) for k in range(2)]
        xt = [sb("xt%d" % k, [128, D], F32) for k in range(2)]
        po = [ps("po%d" % k, [128, 512], F32) for k in range(2)]
        if final:
            fin = sb("fin", [128, D], F32)
            P.dma("sp", fin[:], C.fin)
            epsb = sb("epsb", [128, 1], F32)
            P.memset("dve", epsb[:], EPS)
            junk = sb("junk", [128, D], BF16)
            st = sb("st", [128, 4], F32)
        nx = 0
        P.dma("sp", yT[0][:], C.yT[:, 0:512].rearrange("(c p) n -> p c n", p=128))
        for q in range(NB):
            qs = slice(q * 512, (q + 1) * 512)
            if q + 1 < NB:
                P.dma("sp", yT[(q + 1) % 2][:], C.yT[:, (q + 1) * 512:(q + 2) * 512].rearrange("(c p) n -> p c n", p=128))
            for j in range(4):
                x_t = xt[nx % 2]
                tsl = slice(q * 512 + j * 128, q * 512 + (j + 1) * 128)
                P.dma("sp", x_t[:], xin[tsl, :])
                for hf in range(2):
                    pp = po[hf]
                    for fc in range(NFC):
                        P.mm(pp[:], yT[q % 2][:, fc, j * 128:(j + 1) * 128], wd[:, fc, hf * 512:(hf + 1) * 512], start=(fc == 0), stop=(fc == NFC - 1))
                    P.tt("dve", x_t[:, hf * 512:(hf + 1) * 512], pp[:], x_t[:, hf * 512:(hf + 1) * 512], ALU.add)
                if final:
                    c0 = nx % 2
                    P.act(junk[:], x_t[:], AF.Square, accum_out=st[:, c0:c0 + 1])
                    P.act(st[:, 2 + c0:3 + c0], st[:, c0:c0 + 1], AF.Sqrt, bias=epsb[:], scale=1.0 / D)
                    P.recip(st[:, c0:c0 + 1], st[:, 2 + c0:3 + c0])
                    P.stt("dve", x_t[:], x_t[:], st[:, c0:c0 + 1], fin[:], ALU.mult, ALU.mult)
                P.dma("act", xout[tsl, :], x_t[:])
                nx += 1
        P.flush()


def _rot_consts():
    rc = np.zeros((128, 4), np.float32)
    half_n = 8
    invn = (np.float32(THETA) ** (-(np.arange(half_n, dtype=np.float32)) / np.float32(half_n))).astype(np.float32)
    half_m = 16
    invm = (np.float32(THETA) ** (-(np.arange(half_m, dtype=np.float32)) / np.float32(half_m))).astype(np.float32)
    for r in range(128):
        d = r % 64
        if d < 16:
            rc[r, 0] = invn[d % 8]
            rc[r, 1] = -1.0 if d < 8 else 1.0
        if d < 32:
            rc[r, 2] = invm[d % 16]
            rc[r, 3] = -1.0 if d < 16 else 1.0
    return rc


def _vec_pack(inp, l):
    v = np.zeros((128, NVEC), np.float32)
    v[:, V_MIXN:V_MIXN + 8] = inp["mix_norm"][l].reshape(8, 128).T
    v[:, V_FFNN:V_FFNN + 8] = inp["ffn_norm"][l].reshape(8, 128).T
    v[:, V_QN:V_QN + 3] = inp["mla_q_norm"][l].reshape(3, 128).T
    v[:, V_KVN:V_KVN + 2] = inp["mla_kv_norm"][l].reshape(2, 128).T
    for k, c0 in enumerate((V_CW0, V_CW1, V_CW2)):
        v[:, c0:c0 + NFC] = inp["conv_w"][l, k].reshape(NFC, 128).T
    v[:, V_CB:V_CB + NFC] = inp["conv_b"][l].reshape(NFC, 128).T
    v[0:8, V_BF] = inp["b_forget"][l]
    v[0:64, V_PEK:V_PEK + 32] = inp["cmp_pe_k"][l].T
    v[0:64, V_PEV:V_PEV + 32] = inp["cmp_pe_v"][l].T
    return v


def prep_shared(inp):
    inp = {k: np.asarray(v) for k, v in inp.items()}
    sh = {}
    cols = np.concatenate([np.asarray(g) for g in WIN_GROUPS])
    sh["wext"] = np.ascontiguousarray(inp["w_in"][:, :, cols])
    sh["vecs"] = np.stack([_vec_pack(inp, l) for l in range(L)])
    sh["rotc"] = _rot_consts()
    sh["identf"] = np.eye(128, dtype=np.float32)
    sh["wbr"] = np.ascontiguousarray(np.stack([inp["w_br_nsa"], inp["w_br_fox"], inp["w_br_mla"]], axis=1))
    sh["wout"] = inp["w_out"]
    ucols = []
    for fc in range(NFC):
        ucols += list(range(fc * 128, (fc + 1) * 128)) + list(range(DFF + fc * 128, DFF + (fc + 1) * 128))
    sh["wupx"] = np.ascontiguousarray(inp["w_up"][:, :, np.asarray(ucols)])
    sh["wdown"] = inp["w_down"]
    mk = np.zeros((8, 128, 512), np.float32)
    sl = np.arange(128)[:, None]
    tl = np.arange(512)[None, :]
    for m in range(4):
        vis = (128 * m + sl) <= tl
        mk[m] = np.where(vis, 0.0, NEG)
        mk[4 + m] = np.where(vis, NEG, 0.0)
    sh["maskc"] = np.ascontiguousarray(mk.transpose(1, 0, 2)).astype(ml_dtypes.bfloat16)
    qc = []
    for h in range(8):
        b0 = h * 96
        qc += list(range(b0, b0 + 96)) + list(range(b0, b0 + 64)) + list(range(b0 + 80, b0 + 96)) + list(range(b0 + 64, b0 + 80))
    sh["wuqx"] = np.ascontiguousarray(inp["mla_w_uq"][:, :, np.asarray(qc)])
    kc = []
    for h in range(8):
        kc += list(range(h * 128, h * 128 + 64))
    for h in range(8):
        kc += list(range(h * 128 + 64, h * 128 + 128))
    sh["wukvx"] = np.ascontiguousarray(inp["mla_w_ukv"][:, :, np.asarray(kc)])
    sh["cw1k"], sh["cw1v"], sh["cw2k"], sh["cw2v"] = inp["cmp_w1_k"], inp["cmp_w1_v"], inp["cmp_w2_k"], inp["cmp_w2_v"]
    n = np.arange(256)
    ov = np.zeros((256, 64), np.float32)
    c0 = 16 * n[:, None]
    s0 = 64 * np.arange(64)[None, :]
    ov[:, :] = np.clip(np.minimum(c0 + 32, s0 + 64) - np.maximum(c0, s0), 0, None) / 32.0
    ov[255] = 0.0
    sh["ovl"] = ov
    t = np.arange(S)
    cm = np.where((16 * n[:, None] + 31) <= t[None, :], 0.0, NEG).astype(np.float32)
    sh["cmpm"] = np.ascontiguousarray(cm.reshape(2, 128, S).transpose(1, 0, 2)).astype(ml_dtypes.bfloat16)
    fc = np.zeros((2, 128, NT, 64), np.float32)
    j = np.arange(64)[None, None, :]
    tt = (np.arange(NT)[None, :, None] * 128 + np.arange(128)[:, None, None])
    cur = tt // 64
    forced = (j == 0) | (j == cur) | (j == cur - 1)
    future = j > cur
    fc[0] = np.where(forced | future, 0.0, 1.0)
    fc[1] = np.where(forced, 1e9, np.where(future, -1e9, 0.0))
    sh["forc"] = fc.reshape(2, 128, NT * 64)
    es_ = np.zeros((64, 32, 128), np.float32)
    for kt in range(32):
        es_[2 * kt, kt, 0:64] = 1.0
        es_[2 * kt + 1, kt, 64:128] = 1.0
    sh["esel"] = np.ascontiguousarray(es_.reshape(64, S)).astype(ml_dtypes.bfloat16)
    sh["fin"] = np.ascontiguousarray(np.broadcast_to(inp["final_norm"][None, :], (128, D))).astype(np.float32)
    return sh


def make_in_maps(inp, ncores=8):
    sh = prep_shared(inp)
    x = np.asarray(inp["x"])
    pos = np.asarray(inp["positions"]).astype(np.int32)
    maps = []
    for b in range(ncores):
        m = dict(sh)
        m["x"] = np.ascontiguousarray(x[b])
        m["pos"] = np.ascontiguousarray(pos[b:b + 1])
        maps.append(m)
    return maps


_NC_CACHE = {}
KDBG = {}


def kernel(**inputs):
    if "nc" not in _NC_CACHE:
        _NC_CACHE["nc"] = build_program(dict(KDBG))
    nc = _NC_CACHE["nc"]
    maps = make_in_maps(inputs, 8)
    res = run_bass_kernel_spmd(nc, maps, core_ids=list(range(8)))
    return np.stack([np.asarray(r["y"]) for r in res.results]).astype(np.float32)
```
